# Optimizing a Trainium2 kernel written in Bass

```python
import functools
import jax, jax.numpy as jnp
from jax import lax
import numpy as np

D_MODEL = 1024
BATCH = 8
SEQ = 2048
DEPTH = 1
DEC_BATCH = 32
DEC_SEQ = 4
PAST_LEN = 8192
PAGE_SIZE = 128

N_HEADS_ATTN = 16
HEAD_DIM_ATTN = 64
ATTN_WIDTH = N_HEADS_ATTN * HEAD_DIM_ATTN
Q_BLOCK = 128
D_INNER = 2 * D_MODEL
HEAD_DIM_SSM = 64
N_HEADS_SSM = D_INNER // HEAD_DIM_SSM
N_GROUPS_SSM = 4
D_STATE = 128
CONV_WIDTH_SSM = 4
CONV_DIM_SSM = D_INNER + 2 * N_GROUPS_SSM * D_STATE
SSD_CHUNK = 128
D_FF = 2816
CONV_WIDTH_FFN = 3
D_PLE = 256
RMS_EPS = 1e-6
IN_SIZES = (ATTN_WIDTH, ATTN_WIDTH, ATTN_WIDTH, N_HEADS_ATTN, D_INNER, CONV_DIM_SSM, N_HEADS_SSM, 2 * D_MODEL)
D_IN_PROJ = 3 * ATTN_WIDTH + N_HEADS_ATTN + D_INNER + CONV_DIM_SSM + N_HEADS_SSM + 2 * D_MODEL

kernel_name = 'fox_ssd_parallel_hybrid_step'


def rmsnorm(x, g):
    x32 = x.astype(jnp.float32)
    y = x32 * lax.rsqrt(jnp.mean(x32 * x32, axis=-1, keepdims=True) + RMS_EPS)
    return (y * g.astype(jnp.float32)).astype(x.dtype)


def split_cols(x, sizes):
    outs, start = [], 0
    for s in sizes:
        outs.append(x[..., start:start + s])
        start += s
    return outs


def causal_dwconv(xp, w, b, t):
    width = w.shape[0]
    out = xp[:, 0:t] * w[0]
    for i in range(1, width):
        out = out + xp[:, i:i + t] * w[i]
    return out + b


def fox_logits(q, k, cq, ck, qpos, kpos):
    s = jnp.einsum('bqhd,bkhd->bhqk', q, k).astype(jnp.float32) * (HEAD_DIM_ATTN ** -0.5)
    bias = jnp.swapaxes(cq, 1, 2)[:, :, :, None] - jnp.swapaxes(ck, 1, 2)[:, :, None, :]
    return jnp.where(kpos[None, :] <= qpos[:, None], s + bias, -jnp.inf)


def fox_prompt(q, k, v, logf):
    b, t, h, d = q.shape
    c = jnp.cumsum(logf, axis=1)
    nb = t // Q_BLOCK
    pos = jnp.arange(t, dtype=jnp.int32)
    qb = q.reshape(b, nb, Q_BLOCK, h, d).swapaxes(0, 1)
    cqb = c.reshape(b, nb, Q_BLOCK, h).swapaxes(0, 1)
    pb = pos.reshape(nb, Q_BLOCK)

    def one_block(xs):
        qi, ci, pi = xs
        pr = jax.nn.softmax(fox_logits(qi, k, ci, c, pi, pos), axis=-1).astype(v.dtype)
        return jnp.einsum('bhqk,bkhd->bqhd', pr, v)

    o = lax.map(one_block, (qb, cqb, pb))
    return o.swapaxes(0, 1).reshape(b, t, h, d)


def fox_sample(q, k, v, logf, k_pool, v_pool, lf_pool, page_table, layer):
    b, t, h, d = q.shape
    past = page_table.shape[1] * k_pool.shape[2]
    k_past = k_pool[layer, page_table].reshape(b, past, h, d)
    v_past = v_pool[layer, page_table].reshape(b, past, h, d)
    lf_past = lf_pool[layer, page_table].reshape(b, past, h).astype(jnp.float32)
    c_past = jnp.cumsum(lf_past, axis=1)
    c_past = c_past - c_past[:, -1:]
    c_new = jnp.cumsum(logf, axis=1)
    qpos = past + jnp.arange(t, dtype=jnp.int32)
    kpos_past = jnp.arange(past, dtype=jnp.int32)
    s = jnp.concatenate([fox_logits(q, k_past, c_new, c_past, qpos, kpos_past),
                         fox_logits(q, k, c_new, c_new, qpos, qpos)], axis=-1)
    pr = jax.nn.softmax(s, axis=-1).astype(v.dtype)
    return (jnp.einsum('bhqk,bkhd->bqhd', pr[..., :past], v_past)
            + jnp.einsum('bhqk,bkhd->bqhd', pr[..., past:], v))


def ssd_chunked(xh, dt, a, bm, cm, h0, chunk):
    b_, l_, h_, p_ = xh.shape
    g_, n_ = bm.shape[2], bm.shape[3]
    r_ = h_ // g_
    c_ = l_ // chunk
    xdt = (xh.astype(jnp.float32) * dt[..., None]).reshape(b_, c_, chunk, g_, r_, p_)
    da = (dt * a).reshape(b_, c_, chunk, g_, r_)
    bc = bm.astype(jnp.float32).reshape(b_, c_, chunk, g_, n_)
    cc = cm.astype(jnp.float32).reshape(b_, c_, chunk, g_, n_)
    cs = jnp.cumsum(da, axis=2)
    causal = jnp.tril(jnp.ones((chunk, chunk), dtype=bool))
    seg = cs[:, :, :, None] - cs[:, :, None, :]
    lmat = jnp.exp(jnp.where(causal[None, None, :, :, None, None], seg, -jnp.inf))
    cb = jnp.einsum('bclgn,bcsgn->bclsg', cc, bc)
    y_diag = jnp.einsum('bclsgr,bcsgrp->bclgrp', cb[..., None] * lmat, xdt)
    decay_states = jnp.exp(cs[:, :, -1:] - cs)
    states = jnp.einsum('bclgn,bclgrp->bcgrpn', bc, xdt * decay_states[..., None])
    chunk_decay = jnp.exp(cs[:, :, -1])

    def step(hc, inp):
        dec, st = inp
        return hc * dec[..., None, None] + st, hc

    h_init = h0.astype(jnp.float32).reshape(b_, g_, r_, p_, n_)
    h_final, h_prev = lax.scan(step, h_init, (jnp.moveaxis(chunk_decay, 1, 0), jnp.moveaxis(states, 1, 0)))
    h_prev = jnp.moveaxis(h_prev, 0, 1)
    y_off = jnp.einsum('bclgn,bcgrpn->bclgrp', cc, h_prev) * jnp.exp(cs)[..., None]
    y = (y_diag + y_off).reshape(b_, l_, h_, p_)
    return y, h_final.reshape(b_, h_, p_, n_)


def hybrid_layer(x, p_l, attend, conv_ssm_buf, ssm_h0, conv_ffn_buf, chunk, lw):
    b, t, _ = x.shape
    f32 = jnp.float32
    u = rmsnorm(x, lw['norm_mix'])
    q, k, v, f_logit, z, xbc, dt_raw, gate_logit = split_cols(u @ lw['w_in'], IN_SIZES)
    q = q.reshape(b, t, N_HEADS_ATTN, HEAD_DIM_ATTN)
    k = k.reshape(b, t, N_HEADS_ATTN, HEAD_DIM_ATTN)
    v = v.reshape(b, t, N_HEADS_ATTN, HEAD_DIM_ATTN)
    logf = jax.nn.log_sigmoid(f_logit.astype(f32) + lw['b_f'].astype(f32))
    attn = attend(q, k, v, logf).reshape(b, t, ATTN_WIDTH)
    xbc_pad = jnp.concatenate([conv_ssm_buf.astype(xbc.dtype), xbc], axis=1)
    new_conv_ssm = xbc_pad[:, -(CONV_WIDTH_SSM - 1):]
    xbc_c = jax.nn.silu(causal_dwconv(xbc_pad, lw['conv_ssm_w'], lw['conv_ssm_b'], t))
    xs_, bm, cm = split_cols(xbc_c, (D_INNER, N_GROUPS_SSM * D_STATE, N_GROUPS_SSM * D_STATE))
    xh = xs_.reshape(b, t, N_HEADS_SSM, HEAD_DIM_SSM)
    dt = jax.nn.softplus(dt_raw.astype(f32) + lw['dt_bias'].astype(f32))
    a = -jnp.exp(lw['a_log'].astype(f32))
    y, h_new = ssd_chunked(xh, dt, a, bm.reshape(b, t, N_GROUPS_SSM, D_STATE),
                           cm.reshape(b, t, N_GROUPS_SSM, D_STATE), ssm_h0, chunk)
    y = y + lw['d_skip'].astype(f32)[:, None] * xh.astype(f32)
    yz = (y.reshape(b, t, D_INNER) * jax.nn.silu(z.astype(f32))).reshape(b, t, N_GROUPS_SSM, D_INNER // N_GROUPS_SSM)
    yz = yz * lax.rsqrt(jnp.mean(yz * yz, axis=-1, keepdims=True) + RMS_EPS)
    ssm = (yz.reshape(b, t, D_INNER) * lw['norm_ssm'].astype(f32)).astype(x.dtype)
    g_attn, g_ssm = split_cols(jax.nn.sigmoid(gate_logit), (D_MODEL, D_MODEL))
    merged = g_attn * (attn @ lw['w_o_attn']) + g_ssm * (ssm @ lw['w_o_ssm'])
    x = x + merged @ lw['w_out']
    u2 = rmsnorm(x, lw['norm_ffn'])
    a_up, v_up = split_cols(u2 @ lw['w_up'], (D_FF, D_FF))
    a_pad = jnp.concatenate([conv_ffn_buf.astype(a_up.dtype), a_up], axis=1)
    new_conv_ffn = a_pad[:, -(CONV_WIDTH_FFN - 1):]
    hid = jax.nn.silu(causal_dwconv(a_pad, lw['conv_ffn_w'], lw['conv_ffn_b'], t)) * v_up
    x = x + hid @ lw['w_down']
    ple_gate = jax.nn.sigmoid(rmsnorm(x, lw['norm_ple']) @ lw['w_ple_gate'])
    x = x + ple_gate * (p_l @ lw['w_ple_in'])
    return x, (k, v, logf, h_new, new_conv_ssm, new_conv_ffn)


def setup_inputs(seed: int = 0) -> dict:
    key = jax.random.key(seed)
    ks = jax.random.split(key, 40)
    f32 = jnp.float32

    def nrm(k, shape, scale):
        return jax.random.normal(k, shape, f32) * scale

    n_pages = PAST_LEN // PAGE_SIZE
    n_used = DEC_BATCH * n_pages
    n_pool = n_used + (n_used + 3) // 4
    page_table = jax.random.permutation(ks[0], n_pool)[:n_used].reshape(DEC_BATCH, n_pages).astype(jnp.int32)
    dt0 = jnp.exp(jax.random.uniform(ks[1], (DEPTH, N_HEADS_SSM), f32) * (np.log(0.1) - np.log(0.001)) + np.log(0.001))
    dt_bias = dt0 + jnp.log(-jnp.expm1(-dt0))
    a_log = jnp.log(jax.random.uniform(ks[2], (DEPTH, N_HEADS_SSM), f32, 1.0, 16.0))
    return {
        'x_prompt': nrm(ks[3], (BATCH, SEQ, D_MODEL), 1.0),
        'x_sample': nrm(ks[4], (DEC_BATCH, DEC_SEQ, D_MODEL), 1.0),
        'cache_k': nrm(ks[5], (DEPTH, n_pool, PAGE_SIZE, N_HEADS_ATTN, HEAD_DIM_ATTN), 1.0),
        'cache_v': nrm(ks[6], (DEPTH, n_pool, PAGE_SIZE, N_HEADS_ATTN, HEAD_DIM_ATTN), 1.0),
        'cache_logf': jax.nn.log_sigmoid(nrm(ks[7], (DEPTH, n_pool, PAGE_SIZE, N_HEADS_ATTN), 1.0) + 4.0),
        'state_ssm': nrm(ks[8], (DEPTH, DEC_BATCH, N_HEADS_SSM, HEAD_DIM_SSM, D_STATE), 0.5),
        'state_conv_ssm': nrm(ks[9], (DEPTH, DEC_BATCH, CONV_WIDTH_SSM - 1, CONV_DIM_SSM), 1.0),
        'state_conv_ffn': nrm(ks[10], (DEPTH, DEC_BATCH, CONV_WIDTH_FFN - 1, D_FF), 1.0),
        'page_table': page_table,
        'p_prompt': nrm(ks[11], (DEPTH, BATCH, SEQ, D_PLE), 1.0),
        'p_sample': nrm(ks[12], (DEPTH, DEC_BATCH, DEC_SEQ, D_PLE), 1.0),
        'norm_mix': 1.0 + nrm(ks[13], (DEPTH, D_MODEL), 0.02),
        'w_in': nrm(ks[14], (DEPTH, D_MODEL, D_IN_PROJ), D_MODEL ** -0.5),
        'b_f': 4.0 + nrm(ks[15], (DEPTH, N_HEADS_ATTN), 0.5),
        'conv_ssm_w': nrm(ks[16], (DEPTH, CONV_WIDTH_SSM, CONV_DIM_SSM), CONV_WIDTH_SSM ** -0.5),
        'conv_ssm_b': nrm(ks[17], (DEPTH, CONV_DIM_SSM), 0.02),
        'dt_bias': dt_bias,
        'a_log': a_log,
        'd_skip': 1.0 + nrm(ks[18], (DEPTH, N_HEADS_SSM), 0.1),
        'norm_ssm': 1.0 + nrm(ks[19], (DEPTH, D_INNER), 0.02),
        'w_o_attn': nrm(ks[20], (DEPTH, ATTN_WIDTH, D_MODEL), ATTN_WIDTH ** -0.5),
        'w_o_ssm': nrm(ks[21], (DEPTH, D_INNER, D_MODEL), D_INNER ** -0.5),
        'w_out': nrm(ks[22], (DEPTH, D_MODEL, D_MODEL), D_MODEL ** -0.5),
        'norm_ffn': 1.0 + nrm(ks[23], (DEPTH, D_MODEL), 0.02),
        'w_up': nrm(ks[24], (DEPTH, D_MODEL, 2 * D_FF), D_MODEL ** -0.5),
        'conv_ffn_w': nrm(ks[25], (DEPTH, CONV_WIDTH_FFN, D_FF), CONV_WIDTH_FFN ** -0.5),
        'conv_ffn_b': nrm(ks[26], (DEPTH, D_FF), 0.02),
        'w_down': nrm(ks[27], (DEPTH, D_FF, D_MODEL), D_FF ** -0.5),
        'norm_ple': 1.0 + nrm(ks[28], (DEPTH, D_MODEL), 0.02),
        'w_ple_gate': nrm(ks[29], (DEPTH, D_MODEL, D_MODEL), D_MODEL ** -0.5),
        'w_ple_in': nrm(ks[30], (DEPTH, D_PLE, D_MODEL), D_PLE ** -0.5),
        'norm_final': 1.0 + nrm(ks[31], (D_MODEL,), 0.02),
    }


def reference(x_prompt, x_sample, cache_k, cache_v, cache_logf, state_ssm, state_conv_ssm, state_conv_ffn,
              page_table, p_prompt, p_sample,
              norm_mix, w_in, b_f, conv_ssm_w, conv_ssm_b, dt_bias, a_log, d_skip, norm_ssm,
              w_o_attn, w_o_ssm, w_out, norm_ffn, w_up, conv_ffn_w, conv_ffn_b, w_down,
              norm_ple, w_ple_gate, w_ple_in, norm_final):
    xp, xs = x_prompt, x_sample
    bp = xp.shape[0]
    st_prompt, st_sample = [], []
    for l in range(DEPTH):
        lw = {'norm_mix': norm_mix[l], 'w_in': w_in[l], 'b_f': b_f[l], 'conv_ssm_w': conv_ssm_w[l],
              'conv_ssm_b': conv_ssm_b[l], 'dt_bias': dt_bias[l], 'a_log': a_log[l], 'd_skip': d_skip[l],
              'norm_ssm': norm_ssm[l], 'w_o_attn': w_o_attn[l], 'w_o_ssm': w_o_ssm[l], 'w_out': w_out[l],
              'norm_ffn': norm_ffn[l], 'w_up': w_up[l], 'conv_ffn_w': conv_ffn_w[l], 'conv_ffn_b': conv_ffn_b[l],
              'w_down': w_down[l], 'norm_ple': norm_ple[l], 'w_ple_gate': w_ple_gate[l], 'w_ple_in': w_ple_in[l]}
        xp, stp = hybrid_layer(
            xp, p_prompt[l], fox_prompt,
            jnp.zeros((bp, CONV_WIDTH_SSM - 1, CONV_DIM_SSM), xp.dtype),
            jnp.zeros((bp, N_HEADS_SSM, HEAD_DIM_SSM, D_STATE), jnp.float32),
            jnp.zeros((bp, CONV_WIDTH_FFN - 1, D_FF), xp.dtype),
            min(SSD_CHUNK, xp.shape[1]), lw)
        st_prompt.append(stp)
        attend_s = functools.partial(fox_sample, k_pool=cache_k, v_pool=cache_v, lf_pool=cache_logf,
                                     page_table=page_table, layer=l)
        xs, sts = hybrid_layer(xs, p_sample[l], attend_s, state_conv_ssm[l], state_ssm[l], state_conv_ffn[l],
                               xs.shape[1], lw)
        st_sample.append(sts)
    y_prompt = rmsnorm(xp, norm_final)
    y_sample = rmsnorm(xs, norm_final)
    k_p, v_p, lf_p, h_p, cs_p, cf_p = [jnp.stack(z) for z in zip(*st_prompt)]
    k_s, v_s, lf_s, h_s, cs_s, cf_s = [jnp.stack(z) for z in zip(*st_sample)]
    return (y_prompt, y_sample, k_p, v_p, lf_p, h_p, cs_p, cf_p, k_s, v_s, lf_s, h_s, cs_s, cf_s)
```

```python
import os
import numpy as np
from contextlib import ExitStack
import concourse.bass as bass
import concourse.mybir as mybir
from concourse.bass_utils import run_bass_kernel_spmd

F32 = mybir.dt.float32
BF16 = mybir.dt.bfloat16
I32 = mybir.dt.int32
ALU = mybir.AluOpType
AF = mybir.ActivationFunctionType
AX = mybir.AxisListType

D = 1024; KC = 8; HA = 16; DH = 64; DI = 2048; HS = 32; G = 4; NS = 128
CD = 3072; DFF = 2816; FC = 22; DPLE = 256
Q0 = 0; K0 = 1024; V0 = 2048; F0 = 3072; Z0 = 3088; X0 = 5136; B0 = X0 + 2048; C0 = B0 + 512
DT0 = 8208; GA0 = 8240; DIN = 10288
EPS = 1e-6
C_ID, C_TRI, C_ONES, C_STRICT, C_TRIB, C_STRICTB, C_ONESB = [i * 128 for i in range(7)]
C_VALID = 7 * 128
C_BD = C_VALID + 1
C_BLK = C_BD + 16
C_IOTA = C_BLK + 4
C_PGLT = C_IOTA + 1
C_PGSAME = C_PGLT + 128
C_F32W = C_PGSAME + 128
C_CAUS = C_F32W
C_COLSEL = C_CAUS + 256
C_W = C_COLSEL + 512
C_BFW = 384 + (C_W - C_F32W)
P_GMIX, P_GFFN, P_GPLE, P_GFIN = 0, 8, 16, 24
P_CSW = 32
P_CSB = P_CSW + 96
P_CFW = P_CSB + 24
P_CFB = P_CFW + 66
P_W = P_CFB + 22
R_BF, R_DTB, R_ALOG, R_DSK, R_NSSM = 0, 16, 48, 80, 112
R_W = 112 + 2048


class Ctx:
    def __init__(self, nc, es):
        self.nc = nc
        self.E = {'pe': nc.tensor, 'dve': nc.vector, 'act': nc.scalar, 'pool': nc.gpsimd, 'sp': nc.sync}
        self.S = {}
        self.cnt = {}
        for e in self.E:
            self.S[e] = es.enter_context(nc.semaphore("s_" + e))
            self.cnt[e] = 0
        self.ring = {}
        for q in ('sp', 'pool'):
            self.ring[q] = []
            for i in range(8):
                k = "d_%s%d" % (q, i)
                self.S[k] = es.enter_context(nc.semaphore(k))
                self.cnt[k] = 0
                self.ring[q].append(k)
        self.rr = {'sp': 0, 'pool': 0}
        self.lastw = {}
        self.readers = {}
        self.waited = {e: {} for e in self.E}

    def _wait(self, eng, evs):
        for (k, v) in evs:
            if k == eng and eng == 'pe':
                continue
            if self.waited[eng].get(k, 0) >= v:
                continue
            self.E[eng].wait_ge(self.S[k], v)
            self.waited[eng][k] = v

    def _deps(self, reads, writes):
        evs = []
        for r in reads:
            if r in self.lastw:
                evs.append(self.lastw[r])
        for w in writes:
            if w in self.lastw:
                evs.append(self.lastw[w])
            evs.extend(self.readers.get(w, {}).items())
        return evs

    def _record(self, ev, reads, writes):
        for r in reads:
            d = self.readers.setdefault(r, {})
            if d.get(ev[0], 0) < ev[1]:
                d[ev[0]] = ev[1]
        for w in writes:
            self.lastw[w] = ev
            self.readers[w] = {}

    def op(self, eng, fn, reads=(), writes=()):
        self._wait(eng, self._deps(reads, writes))
        ins = fn(self.E[eng])
        self.cnt[eng] += 1
        ins.then_inc(self.S[eng], 1)
        self._record((eng, self.cnt[eng]), reads, writes)

    def dma(self, q, fn, reads=(), writes=()):
        k = self.ring[q][self.rr[q] % 8]
        self.rr[q] += 1
        evs = self._deps(reads, writes)
        if self.cnt[k] > 0:
            evs.append((k, self.cnt[k]))
        self._wait(q, evs)
        ins = fn(self.E[q])
        self.cnt[k] += 16
        ins.then_inc(self.S[k], 16)
        self._record((k, self.cnt[k]), reads, writes)

    def barrier(self):
        evs = [(k, v) for k, v in self.cnt.items() if v > 0]
        for e in self.E:
            self._wait(e, evs)

    def final(self):
        evs = [(k, v) for k, v in self.cnt.items() if v > 0]
        self._wait('sp', evs)


def build(T, NPG, NPOOL):
    NT = T + 128
    NTL = NT // 128
    PTL = T // 128
    QB = min(512, T)
    NQB = T // QB
    blocks = []
    c0 = 0
    while c0 < T:
        blocks.append((c0, QB)); c0 += QB
    blocks.append((T, 128))
    NPAGE = 4 * NPG

    nc = bass.Bass("TRN2", target_bir_lowering=False)
    di = lambda n, s, dt=F32: nc.dram_tensor(n, s, dt, kind="ExternalInput").ap()
    do = lambda n, s, dt=F32: nc.dram_tensor(n, s, dt, kind="ExternalOutput").ap()
    xT_d = di("xT", [D, NT]); pT_d = di("pT", [DPLE, NT])
    kpool_d = di("kpool", [NPOOL * 128, 1024]); vpool_d = di("vpool", [NPOOL * 128, 1024])
    lfpool2_d = di("lfpool2", [NPOOL, 2048])
    ptab_d = di("ptab", [1, NPAGE], I32)
    hT0_d = di("hT0", [4, 128, DI]); cs0_d = di("cs0", [CD, 4, 3]); cf0_d = di("cf0", [DFF, 4, 2])
    consts_d = di("consts", [128, C_W]); pfm_d = di("pvec_fm", [128, P_W]); prow_d = di("pvec_row", [1, R_W])
    w_in_d = di("w_in", [D, DIN]); woa_d = di("w_o_attn", [D, D]); wos_d = di("w_o_ssm", [DI, D])
    wout_d = di("w_out", [D, D]); wup_d = di("w_up", [D, 2 * DFF]); wdn_d = di("w_down", [DFF, D])
    wpg_d = di("w_ple_gate", [D, D]); wpi_d = di("w_ple_in", [DPLE, D])
    yT_o = do("yT", [D, NT]); k_o = do("k_tm", [NT, 1024]); v_o = do("v_tm", [NT, 1024]); lf_o = do("logf", [NT, 16])
    hTp_o = do("hT_p", [128, DI]); hTs_o = do("hT_s", [4, 128, DI])
    xbc_o = do("xbc_tm", [256, CD]); aup_o = do("aup_tm", [256, DFF])

    es = ExitStack()
    C = Ctx(nc, es)
    STOP = int(os.environ.get("KSTOP", "99"))

    def stop_here():
        C.final()
        es.close()
        return nc
    ARENA_W = 51700
    big = es.enter_context(nc.sbuf_tensor("arena", [128, ARENA_W], F32))
    free_l = [[0, ARENA_W]]
    used = {}

    def release(name):
        o, n4 = used.pop(name)
        free_l.append([o, n4]); free_l.sort()
        i = 0
        while i + 1 < len(free_l):
            if free_l[i][0] + free_l[i][1] == free_l[i + 1][0]:
                free_l[i][1] += free_l[i + 1][1]; del free_l[i + 1]
            else:
                i += 1

    def sb(name, shape, dt=F32, st=None):
        isz = 2 if dt == BF16 else 4
        per = 1
        for d_ in shape[1:]:
            per *= d_
        n4 = (per * isz + 3) // 4
        n4 = (n4 + 15) // 16 * 16
        for f_ in free_l:
            if f_[1] >= n4:
                o = f_[0]; f_[0] += n4; f_[1] -= n4
                break
        else:
            raise RuntimeError("arena full allocating %s (%d B); used=%s" % (name, n4 * 4, {k: v[1] * 4 for k, v in used.items()}))
        assert name not in used, name
        used[name] = (o, n4)
        ap = big[0:shape[0], o:o + n4]
        if dt != F32:
            ap = ap.bitcast(dt)
        ap = ap[:, 0:per]
        if len(shape) == 3:
            ap = ap.rearrange("p (a b) -> p a b", a=shape[1])
        elif len(shape) == 4:
            ap = ap.rearrange("p (a b c) -> p a b c", a=shape[1], b=shape[2])
        if st is not None:
            st.callback(release, name)
        return ap
    psf = [es.enter_context(nc.psum_tensor("psf%d" % i, [128, 512], F32)) for i in range(6)]
    psb = [es.enter_context(nc.psum_tensor("psb%d" % i, [128, 1024], BF16)) for i in range(2)]
    PK = lambda i: ('psf', i)
    PB = lambda i: ('psb', i)
    evt = [0]

    def evac(out, in_, reads, writes, func=AF.Copy, scale=1.0):
        evt[0] += 1
        if evt[0] % 2 == 0 or func != AF.Copy or scale != 1.0 or len(out.shape) > 2 or len(in_.shape) > 2:
            C.op('act', lambda e: e.activation(out=out, in_=in_, func=func, scale=scale), reads, writes)
        else:
            C.op('dve', lambda e: e.tensor_scalar(out=out, in0=in_, scalar1=1.0, scalar2=None, op0=ALU.mult), reads, writes)

    cst = sb("cst", [128, C_F32W]); cstb = sb("cstb", [128, C_BFW], BF16)
    pfm = sb("pfm", [128, P_W]); prow = sb("prow", [128, R_W])
    abc = sb("abc", [128, HS])
    uT = sb("uT", [128, KC, NT], BF16)
    lf_tm = sb("lf_tm", [128, NTL, 16])
    C.dma('sp', lambda e: e.dma_start(out=cst[:], in_=consts_d[:, 0:C_F32W]), (), ['cst'])
    C.dma('pool', lambda e: e.dma_start(out=cstb[:, 0:384], in_=consts_d[:, 0:384]), (), ['cstb'])
    C.dma('pool', lambda e: e.dma_start(out=cstb[:, 384:C_BFW], in_=consts_d[:, C_F32W:C_W]), (), ['cstb'])
    C.dma('sp', lambda e: e.dma_start(out=pfm[:], in_=pfm_d), (), ['pfm'])
    C.dma('sp', lambda e: e.dma_start(out=prow[:], in_=prow_d.partition_broadcast(128)), (), ['prow'])
    C.op('act', lambda e: e.activation(out=abc[:], in_=prow[:, R_ALOG:R_ALOG + HS], func=AF.Exp), ['prow'], ['abc'])
    C.op('dve', lambda e: e.tensor_scalar(out=abc[:], in0=abc[:], scalar1=-1.0, scalar2=None, op0=ALU.mult), ['abc'], ['abc'])
    cc = lambda off, w=128, p0=0, p1=128: cst[p0:p1, off:off + w]
    ccb = lambda off, w=128, p0=0, p1=128: cstb[p0:p1, (off if off < 384 else off - C_F32W + 384):(off if off < 384 else off - C_F32W + 384) + w]

    def rms_to_bf(src, src_key, gcol, dst, dst_key, st):
        sqs = [sb("sq%d" % i, [128, KC, QB], BF16) for i in range(2)]
        for bi, (b0, n) in enumerate(blocks):
            xin = src(bi, b0, n)
            sq = sqs[bi % 2]
            C.op('act', lambda e: e.activation(out=sq[:, :, 0:n], in_=xin[0], func=AF.Square), [xin[1]], [('sq', bi % 2)])
            pi = 4 + bi % 2
            for c in range(KC):
                C.op('pe', lambda e: e.matmul(psf[pi][:, 0:n], lhsT=ccb(C_ONES), rhs=sq[:, c, 0:n], start=(c == 0), stop=(c == KC - 1)),
                     [('sq', bi % 2), 'cstb'], [PK(pi)])
            rs = st['rs'][bi % 2]
            C.op('act', lambda e: e.activation(out=rs[:, 0:n], in_=psf[pi][:, 0:n], func=AF.Sqrt, scale=1.0 / D, bias=st['eps'][:, 0:1]),
                 [PK(pi), 'epsc'], [('rs', bi % 2)])
            C.op('dve', lambda e: e.reciprocal(out=rs[:, 0:n], in_=rs[:, 0:n]), [('rs', bi % 2)], [('rs', bi % 2)])
            for c in range(KC):
                eng = 'dve'
                C.op(eng, lambda e: e.scalar_tensor_tensor(out=dst[:, c, b0:b0 + n], in0=xin[0][:, c, :], scalar=pfm[:, gcol + c:gcol + c + 1],
                                                           in1=rs[:, 0:n], op0=ALU.mult, op1=ALU.mult),
                     [xin[1], ('rs', bi % 2), 'pfm'], [(dst_key, bi)])
        C.barrier()
        release("sq0"); release("sq1")

    normst = {'rs': [sb("rs%d" % i, [128, QB]) for i in range(2)], 'eps': sb("epsc", [128, 1])}
    C.op('pool', lambda e: e.memset(normst['eps'][:], EPS), (), ['epsc'])
    xTv = xT_d.rearrange("(c p) n -> p c n", p=128)
    with ExitStack() as ph:
        xst = [sb("xst%d" % i, [128, KC, QB], F32, ph) for i in range(2)]

        def src_x(bi, b0, n):
            C.dma('sp', lambda e: e.dma_start(out=xst[bi % 2][:, :, 0:n], in_=xTv[:, :, b0:b0 + n]), (), [('xst', bi % 2)])
            return (xst[bi % 2][:, :, 0:n], ('xst', bi % 2))
        rms_to_bf(src_x, None, P_GMIX, uT, 'uT', normst)
    C.barrier()
    UT_ALL = [('uT', bi) for bi in range(len(blocks))]

    def wload(dst, wd, r0, nk, c0_, ncol, key):
        C.dma('pool', lambda e: e.dma_start(out=dst[:, 0:nk, 0:ncol],
                                            in_=wd[r0:r0 + nk * 128, c0_:c0_ + ncol].rearrange("(k p) n -> p k n", p=128)), (), [key])

    def fm_proj(w, wkey, wc0, rhsT, rhs_keys, nk, pbank, b0, n):
        for k in range(nk):
            C.op('pe', lambda e: e.matmul(psf[pbank][:, 0:n], lhsT=w[:, k, wc0:wc0 + 128], rhs=rhsT[:, k, b0:b0 + n],
                                          start=(k == 0), stop=(k == nk - 1)), [wkey] + rhs_keys, [PK(pbank)])

    if STOP <= 1:
        return stop_here()
    ssmT = sb("ssmT", [128, 16, NT], BF16)
    with ExitStack() as ph:
        wg = [sb("wg0", [128, KC, 1288], BF16, ph)] * 2
        xpad = sb("xpad", [128, 2, 3 + QB], F32, ph)
        halo = sb("halo", [128, 6, 3], F32, ph)
        xsp = sb("xsp", [128, 6, 4, 7], F32, ph)
        ctmp = [sb("ctmp%d" % i, [128, QB], F32, ph) for i in range(2)]
        xcT = sb("xcT", [128, 6, QB], BF16, ph)
        xs_tm = sb("xs_tm", [128, 512], BF16, ph); b_tm = sb("b_tm", [128, 128], BF16, ph)
        zs = sb("zs", [128, 512], F32, ph)
        rmat = sb("rmat", [128, 8, 128], F32, ph)
        emat = sb("emat", [128, 8, 128], BF16, ph); mT = sb("mT", [128, 8, 128], BF16, ph)
        cbm = sb("cbm", [128, 128], BF16, ph)
        xdt = sb("xdt", [128, 512], BF16, ph); xdd = sb("xdd", [128, 512], BF16, ph)
        yv = sb("yv", [128, 512], F32, ph); y2 = sb("y2", [128, 512], F32, ph); ysq = y2
        ss = sb("ss", [128, 1], F32, ph)
        ssm_tm = sb("ssm_tm", [128, 512], BF16, ph)
        hT = sb("hT", [128, 512], F32, ph); hTb = sb("hTb", [128, 512], BF16, ph)
        hTs = sb("hTs", [128, 4, 512], F32, ph); hTsb = sb("hTsb", [128, 4, 512], BF16, ph)
        dam = sb("dam", [128, 4, 8], F32, ph)
        raw_a = sb("raw_a", [128, NTL, 8], F32, ph); sp_a = sb("sp_a", [128, NTL, 8], F32, ph)
        dtt_a = sb("dtt_a", [128, NTL, 8], F32, ph); da_a = sb("da_a", [128, NTL, 8], F32, ph)
        ecs_a = sb("ecs_a", [128, NTL, 8], F32, ph); dec_a = sb("dec_a", [128, NTL, 8], F32, ph)
        edt_a = sb("edt_a", [128, NTL + 3, 8], F32, ph); dtw_a = sb("dtw_a", [128, NTL, 8], F32, ph)
        ctm = sb("ctm", [128, 4, 128], BF16, ph); xddm = sb("xddm", [128, 512], BF16, ph)
        xbst = rmat.rearrange("p h l -> p (h l)")[:, 0:768]

        def gload(g):
            w = wg[0]; kk = ('wg', 0)
            wload(w[:, :, 0:512], w_in_d, 0, KC, X0 + 512 * g, 512, kk)
            wload(w[:, :, 512:640], w_in_d, 0, KC, B0 + 128 * g, 128, kk)
            wload(w[:, :, 640:768], w_in_d, 0, KC, C0 + 128 * g, 128, kk)
            wload(w[:, :, 768:1280], w_in_d, 0, KC, Z0 + 512 * g, 512, kk)
            wload(w[:, :, 1280:1288], w_in_d, 0, KC, DT0 + 8 * g, 8, kk)
        csw = lambda j, i: pfm[:, P_CSW + j * 4 + i:P_CSW + j * 4 + i + 1]
        for g in range(G):
            gload(g)
            w = wg[0]; wk = ('wg', 0)
            gj = [4 * g + q for q in range(4)] + [16 + g, 20 + g]
            C.op('pool', lambda e: e.memset(halo[:], 0.0), (), ['halo'])
            C.op('pool', lambda e: e.memset(hT[:], 0.0), (), ['hT'])
            C.op('pool', lambda e: e.memset(hTb[:], 0.0), (), ['hTb'])
            C.dma('sp', lambda e: e.dma_start(out=hTs[:], in_=hT0_d[:, :, 512 * g:512 * (g + 1)].rearrange("b n f -> n b f")), (), ['hTs'])
            C.op('act', lambda e: e.activation(out=hTsb[:], in_=hTs[:], func=AF.Copy), ['hTs'], ['hTsb'])
            for q, j in enumerate(gj):
                C.dma('sp', lambda e: e.dma_start(out=xsp[:, q, :, 0:3], in_=cs0_d[j * 128:(j + 1) * 128, :, :]), (), ['xsp'])
            for t in range(NTL):
                pi = t % 2
                for k in range(KC):
                    C.op('pe', lambda e: e.matmul(psf[pi][:, 0:8], lhsT=uT[:, k, t * 128:(t + 1) * 128], rhs=w[:, k, 1280:1288], start=(k == 0), stop=(k == KC - 1)), [wk] + UT_ALL, [PK(pi)])
                C.op('dve', lambda e: e.tensor_tensor(out=raw_a[:, t, :], in0=psf[pi][:, 0:8], in1=prow[:, R_DTB + 8 * g:R_DTB + 8 * g + 8], op=ALU.add), [PK(pi), 'prow'], ['raw_a'])
            C.op('dve', lambda e: e.tensor_scalar(out=sp_a[:], in0=raw_a[:], scalar1=-1.0, scalar2=None, op0=ALU.mult), ['raw_a'], ['sp_a'])
            C.op('dve', lambda e: e.tensor_tensor(out=sp_a[:], in0=sp_a[:], in1=raw_a[:], op=ALU.min), ['raw_a', 'sp_a'], ['sp_a'])
            C.op('act', lambda e: e.activation(out=sp_a[:], in_=sp_a[:], func=AF.Exp), ['sp_a'], ['sp_a'])
            C.op('act', lambda e: e.activation(out=sp_a[:], in_=sp_a[:], func=AF.Ln, bias=1.0), ['sp_a'], ['sp_a'])
            C.op('dve', lambda e: e.scalar_tensor_tensor(out=dtt_a[:], in0=raw_a[:], scalar=0.0, in1=sp_a[:], op0=ALU.max, op1=ALU.add), ['raw_a', 'sp_a'], ['dtt_a'])
            C.op('dve', lambda e: e.tensor_tensor(out=da_a[:], in0=dtt_a[:], in1=abc[:, 8 * g:8 * g + 8].unsqueeze(1).to_broadcast([128, NTL, 8]), op=ALU.mult), ['dtt_a', 'abc'], ['da_a'])
            if os.environ.get("KPRE") == "0" and g >= int(os.environ.get("KG", "0")):
                return stop_here()
            C.op('dve', lambda e: e.tensor_scalar(out=da_a[:, PTL, :], in0=da_a[:, PTL, :], scalar1=cc(C_VALID, 1), scalar2=None, op0=ALU.mult), ['da_a', 'cst'], ['da_a'])
            C.op('dve', lambda e: e.tensor_tensor(out=dam[:], in0=da_a[:, PTL, :].unsqueeze(1).to_broadcast([128, 4, 8]), in1=cc(C_BLK, 4).unsqueeze(2).to_broadcast([128, 4, 8]), op=ALU.mult),
                 ['da_a', 'cst'], ['dam'])
            for t in range(NTL):
                sm_ = (t == PTL)
                TR_, ST_ = (C_TRIB, C_STRICTB) if sm_ else (C_TRI, C_STRICT)
                C.op('pe', lambda e: e.matmul(psf[2][:, 24 * t:24 * t + 8], lhsT=cc(TR_), rhs=da_a[:, t, :], start=True, stop=True), ['da_a', 'cst'], [PK(2)])
                C.op('pe', lambda e: e.matmul(psf[2][:, 24 * t + 8:24 * t + 16], lhsT=cc(ST_), rhs=da_a[:, t, :], start=True, stop=True), ['da_a', 'cst'], [PK(2)])
                if not sm_:
                    C.op('pe', lambda e: e.matmul(psf[2][:, 24 * t + 16:24 * t + 24], lhsT=cc(C_ONES), rhs=da_a[:, t, :], start=True, stop=True), ['da_a', 'cst'], [PK(2)])
                else:
                    C.op('pe', lambda e: e.matmul(psf[2][:, 24 * t + 16:24 * t + 48], lhsT=cc(C_ONES), rhs=dam[:].rearrange("p b h -> p (b h)"), start=True, stop=True), ['dam', 'cst'], [PK(2)])
            pv_ = psf[2][:, 0:24 * NTL].rearrange("p (t c) -> p t c", c=24)
            if os.environ.get("KA3"):
                for t in range(NTL):
                    C.op('act', lambda e: e.activation(out=ecs_a[:, t, :], in_=psf[2][:, 24 * t:24 * t + 8], func=AF.Exp), [PK(2)], ['ecs_a'])
                    C.op('act', lambda e: e.activation(out=dec_a[:, t, :], in_=psf[2][:, 24 * t + 8:24 * t + 16], func=AF.Exp), [PK(2)], ['dec_a'])
                    if t < PTL:
                        C.op('act', lambda e: e.activation(out=edt_a[:, t, :], in_=psf[2][:, 24 * t + 16:24 * t + 24], func=AF.Exp), [PK(2)], ['edt_a'])
            else:
                C.op('act', lambda e: e.activation(out=ecs_a[:], in_=pv_[:, :, 0:8], func=AF.Exp), [PK(2)], ['ecs_a'])
                C.op('act', lambda e: e.activation(out=dec_a[:], in_=pv_[:, :, 8:16], func=AF.Exp), [PK(2)], ['dec_a'])
                C.op('act', lambda e: e.activation(out=edt_a[:, 0:PTL, :], in_=pv_[:, 0:PTL, 16:24], func=AF.Exp), [PK(2)], ['edt_a'])
            C.op('act', lambda e: e.activation(out=edt_a[:, PTL:PTL + 4, :], in_=psf[2][:, 24 * PTL + 16:24 * PTL + 48].rearrange("p (b h) -> p b h", h=8), func=AF.Exp), [PK(2)], ['edt_a'])
            if os.environ.get("KPRE") == "1":
                return stop_here()
            C.op('pool', lambda e: e.tensor_tensor(out=dtw_a[:], in0=dtt_a[:], in1=dec_a[:], op=ALU.mult), ['dtt_a', 'dec_a'], ['dtw_a'])
            if os.environ.get("KPRE") == "2" and g >= int(os.environ.get("KG", "0")):
                return stop_here()
            for bi, (b0, n) in enumerate(blocks):
                if os.environ.get("KB") == "3" and g == 1 and bi == 1:
                    return stop_here()
                samp = (b0 == T)
                if samp:
                    C.op('pool', lambda e: e.memset(xcT[:, :, 0:128], 0.0), (), ['xcT'])
                for q, j in enumerate(gj):
                    pi = q % 4
                    xp_ = xpad[:, q % 2, :]; xk = ('xpad', q % 2)
                    fm_proj(w, wk, q * 128, uT, [('uT', bi)], KC, pi, b0, n)
                    if not samp:
                        C.op('pool', lambda e: e.tensor_copy(out=xp_[:, 0:3], in_=halo[:, q, :]), ['halo'], [xk])
                        evac(xp_[:, 3:3 + n], psf[pi][:, 0:n], [PK(pi)], [xk])
                        C.op('pool', lambda e: e.tensor_copy(out=halo[:, q, :], in_=xp_[:, n:n + 3]), [xk], ['halo'])
                    else:
                        evac(xsp[:, q, :, 3:7], psf[pi][:, 0:128].rearrange("p (b t) -> p b t", t=32)[:, :, 0:4], [PK(pi)], ['xsp'])
                    ct = ctmp[q % 2]; ck = ('ctmp', q % 2)
                    if not samp:
                        xin = lambda i: xp_[:, i:i + n]
                        cto = ct[:, 0:n]
                        xco = xcT[:, q, 0:n]
                    else:
                        xin = lambda i: xsp[:, q, :, i:i + 4]
                        cto = ct[:, 0:16].rearrange("p (b t) -> p b t", t=4)
                        xco = xcT[:, q, 0:128].rearrange("p (b t) -> p b t", t=32)[:, :, 0:4]
                    rk = 'xsp' if samp else xk
                    eng = 'dve'
                    C.op(eng, lambda e: e.tensor_scalar(out=cto, in0=xin(0), scalar1=csw(j, 0), scalar2=pfm[:, P_CSB + j:P_CSB + j + 1], op0=ALU.mult, op1=ALU.add),
                         [rk, 'pfm'], [ck])
                    for i in range(1, 4):
                        C.op(eng, lambda e: e.scalar_tensor_tensor(out=cto, in0=xin(i), scalar=csw(j, i), in1=cto, op0=ALU.mult, op1=ALU.add), [rk, ck, 'pfm'], [ck])
                    C.op('act', lambda e: e.activation(out=xco, in_=cto, func=AF.Silu), [ck], ['xcT'])
                if bi >= len(blocks) - 2:
                    tcol = b0 + n - 128
                    orow = 0 if not samp else 128
                    for k in range(KC):
                        C.op('pe', lambda e: e.matmul(psf[4][:, 0:512], lhsT=uT[:, k, tcol:tcol + 128], rhs=w[:, k, 0:512], start=(k == 0), stop=(k == KC - 1)), [wk, ('uT', bi)], [PK(4)])
                    for k in range(KC):
                        C.op('pe', lambda e: e.matmul(psf[5][:, 0:256], lhsT=uT[:, k, tcol:tcol + 128], rhs=w[:, k, 512:768], start=(k == 0), stop=(k == KC - 1)), [wk, ('uT', bi)], [PK(5)])
                    evac(xbst[:, 0:512], psf[4][:, 0:512], [PK(4)], ['rmat'])
                    evac(xbst[:, 512:768], psf[5][:, 0:256], [PK(5)], ['rmat'])
                    C.dma('sp', lambda e: e.dma_start(out=xbc_o[orow:orow + 128, 512 * g:512 * (g + 1)], in_=xbst[:, 0:512]), ['rmat'], [])
                    C.dma('sp', lambda e: e.dma_start(out=xbc_o[orow:orow + 128, 2048 + 128 * g:2048 + 128 * (g + 1)], in_=xbst[:, 512:640]), ['rmat'], [])
                    C.dma('sp', lambda e: e.dma_start(out=xbc_o[orow:orow + 128, 2560 + 128 * g:2560 + 128 * (g + 1)], in_=xbst[:, 640:768]), ['rmat'], [])
                if os.environ.get("KB") == "1" and g == 1 and bi == 0:
                    return stop_here()
                if os.environ.get("KB") == "2" and g == 1 and bi == 1:
                    return stop_here()
                for ci in range(n // 128):
                    l0 = ci * 128
                    t0 = b0 + l0
                    TRI_, STR_, ONE_ = (C_TRIB, C_STRICTB, C_ONESB) if samp else (C_TRI, C_STRICT, C_ONES)
                    for k in range(KC):
                        C.op('pe', lambda e: e.matmul(psf[0][:, 0:512], lhsT=uT[:, k, t0:t0 + 128], rhs=w[:, k, 768:1280], start=(k == 0), stop=(k == KC - 1)), [wk, ('uT', bi)], [PK(0)])
                    C.op('act', lambda e: e.activation(out=zs[:], in_=psf[0][:, 0:512], func=AF.Silu), [PK(0)], ['zs'])
                    tt_ = t0 // 128
                    dtt = dtt_a[:, tt_, :]; da = da_a[:, tt_, :]; ecs = ecs_a[:, tt_, :]; dtw = dtw_a[:, tt_, :]
                    edt = edt_a[:, tt_:tt_ + 4, :]
                    if os.environ.get("KCH") == "A" and g >= int(os.environ.get("KG", "0")):
                        return stop_here()
                    pb_ = ci % 2
                    for q in range(5):
                        C.op('pe', lambda e: e.transpose(psb[pb_][:, q * 128:(q + 1) * 128], xcT[:, q, l0:l0 + 128], ccb(C_ID)), ['xcT', 'cstb'], [PB(pb_)])
                    C.op('act', lambda e: e.activation(out=xs_tm[:], in_=psb[pb_][:, 0:512], func=AF.Copy), [PB(pb_)], ['xs_tm'])
                    C.op('act', lambda e: e.activation(out=b_tm[:], in_=psb[pb_][:, 512:640], func=AF.Copy), [PB(pb_)], ['b_tm'])
                    if os.environ.get("KCH") == "B" and g >= int(os.environ.get("KG", "0")):
                        return stop_here()
                    xs3 = xs_tm[:].rearrange("p (h d) -> p h d", d=64)
                    C.op('dve', lambda e: e.tensor_tensor(out=xdt[:].rearrange("p (h d) -> p h d", d=64), in0=xs3, in1=dtt.unsqueeze(2).to_broadcast([128, 8, 64]), op=ALU.mult),
                         ['xs_tm', 'dtt_a'], ['xdt'])
                    if os.environ.get("KCH") == "C" and g >= int(os.environ.get("KG", "0")):
                        return stop_here()
                    C.op('pool', lambda e: e.tensor_tensor(out=xdd[:].rearrange("p (h d) -> p h d", d=64), in0=xs3, in1=dtw.unsqueeze(2).to_broadcast([128, 8, 64]), op=ALU.mult),
                         ['xs_tm', 'dtw_a'], ['xdd'])
                    if os.environ.get("KCH") == "1" and g >= int(os.environ.get("KG", "0")):
                        return stop_here()
                    C.op('pe', lambda e: e.matmul(psf[3][:, 0:128], lhsT=xcT[:, 4, l0:l0 + 128], rhs=xcT[:, 5, l0:l0 + 128], start=True, stop=True), ['xcT'], [PK(3)])
                    C.op('dve', lambda e: e.tensor_tensor(out=cbm[:], in0=psf[3][:, 0:128], in1=cc(TRI_), op=ALU.mult), [PK(3), 'cst'], ['cbm'])
                    C.op(os.environ.get('KRM', 'pool'), lambda e: e.tensor_tensor(out=rmat[:], in0=da.unsqueeze(2).to_broadcast([128, 8, 128]), in1=cc(TRI_).unsqueeze(1).to_broadcast([128, 8, 128]), op=ALU.mult),
                         ['da_a', 'cst'], ['rmat'])
                    rm2 = rmat[:].rearrange("p h l -> p (h l)")
                    for hh in range(2):
                        C.op('pe', lambda e: e.matmul(psf[4 + hh][:, 0:512], lhsT=cc(STR_), rhs=rm2[:, 512 * hh:512 * (hh + 1)], start=True, stop=True), ['rmat', 'cst'], [PK(4 + hh)])
                        C.op('act', lambda e: e.activation(out=emat[:].rearrange("p h l -> p (h l)")[:, 512 * hh:512 * (hh + 1)], in_=psf[4 + hh][:, 0:512], func=AF.Exp), [PK(4 + hh)], ['emat'])
                    C.op('dve', lambda e: e.tensor_tensor(out=mT[:], in0=emat[:], in1=cbm[:].unsqueeze(1).to_broadcast([128, 8, 128]), op=ALU.mult), ['emat', 'cbm'], ['mT'])
                    if os.environ.get("KCH") == "2" and g >= int(os.environ.get("KG", "0")):
                        return stop_here()
                    for h in range(8):
                        C.op('pe', lambda e: e.matmul(psf[0][:, 64 * h:64 * h + 64], lhsT=mT[:, h, :], rhs=xdt[:, 64 * h:64 * h + 64], start=True, stop=True), ['mT', 'xdt'], [PK(0)])
                    if not samp:
                        C.op('pe', lambda e: e.matmul(psf[1][:, 0:512], lhsT=xcT[:, 5, l0:l0 + 128], rhs=hTb[:], start=True, stop=True), ['xcT', 'hTb'], [PK(1)])
                    else:
                        C.op('dve', lambda e: e.tensor_tensor(out=ctm[:], in0=xcT[:, 5, 0:128].unsqueeze(1).to_broadcast([128, 4, 128]),
                                                              in1=ccb(C_COLSEL, 512).rearrange("p (b l) -> p b l", b=4), op=ALU.mult), ['xcT', 'cstb'], ['ctm'])
                        for b in range(4):
                            C.op('pe', lambda e: e.matmul(psf[1][:, 0:512], lhsT=ctm[:, b, :], rhs=hTsb[:, b, :], start=(b == 0), stop=(b == 3)), ['ctm', 'hTsb'], [PK(1)])
                    if os.environ.get("KCH") == "3" and g >= int(os.environ.get("KG", "0")):
                        return stop_here()
                    yv3 = yv[:].rearrange("p (h d) -> p h d", d=64)
                    C.op('dve', lambda e: e.tensor_tensor(out=yv3, in0=psf[1][:, 0:512].rearrange("p (h d) -> p h d", d=64), in1=ecs.unsqueeze(2).to_broadcast([128, 8, 64]), op=ALU.mult),
                         [PK(1), 'ecs_a'], ['yv'])
                    C.op('dve', lambda e: e.tensor_tensor(out=yv[:], in0=yv[:], in1=psf[0][:, 0:512], op=ALU.add), ['yv', PK(0)], ['yv'])
                    C.op('pool', lambda e: e.tensor_tensor(out=y2[:].rearrange("p (h d) -> p h d", d=64), in0=xs3, in1=prow[:, R_DSK + 8 * g:R_DSK + 8 * g + 8].unsqueeze(2).to_broadcast([128, 8, 64]), op=ALU.mult),
                         ['xs_tm', 'prow'], ['y2'])
                    C.op('dve', lambda e: e.tensor_tensor(out=yv[:], in0=yv[:], in1=y2[:], op=ALU.add), ['yv', 'y2'], ['yv'])
                    C.op('dve', lambda e: e.tensor_tensor(out=yv[:], in0=yv[:], in1=zs[:], op=ALU.mult), ['yv', 'zs'], ['yv'])
                    C.op('act', lambda e: e.activation(out=ysq[:], in_=yv[:], func=AF.Square, accum_out=ss[:]), ['yv'], ['y2', 'ss'])
                    C.op('act', lambda e: e.activation(out=ss[:], in_=ss[:], func=AF.Sqrt, scale=1.0 / 512, bias=normst['eps'][:, 0:1]), ['ss', 'epsc'], ['ss'])
                    C.op('dve', lambda e: e.reciprocal(out=ss[:], in_=ss[:]), ['ss'], ['ss'])
                    C.op('dve', lambda e: e.scalar_tensor_tensor(out=ssm_tm[:], in0=yv[:], scalar=ss[:, 0:1], in1=prow[:, R_NSSM + 512 * g:R_NSSM + 512 * (g + 1)], op0=ALU.mult, op1=ALU.mult),
                         ['yv', 'ss', 'prow'], ['ssm_tm'])
                    if os.environ.get("KCH") == "4" and g >= int(os.environ.get("KG", "0")):
                        return stop_here()
                    pb2 = 1 - pb_
                    for q in range(4):
                        C.op('pe', lambda e: e.transpose(psb[pb2][:, q * 128:(q + 1) * 128], ssm_tm[:, q * 128:(q + 1) * 128], ccb(C_ID)), ['ssm_tm', 'cstb'], [PB(pb2)])
                    evac(ssmT[:, 4 * g:4 * g + 4, t0:t0 + 128], psb[pb2][:, 0:512].rearrange("p (q t) -> p q t", t=128), [PB(pb2)], [('ssmT', bi)])
                    if os.environ.get("KCH") == "5" and g >= int(os.environ.get("KG", "0")):
                        return stop_here()
                    if not samp:
                        C.op('pe', lambda e: e.matmul(psf[3][:, 0:512], lhsT=b_tm[:], rhs=xdd[:], start=True, stop=True), ['b_tm', 'xdd'], [PK(3)])
                        C.op('dve', lambda e: e.tensor_tensor(out=hT[:].rearrange("p (h d) -> p h d", d=64), in0=hT[:].rearrange("p (h d) -> p h d", d=64),
                                                              in1=edt[:, 0, :].unsqueeze(2).to_broadcast([128, 8, 64]), op=ALU.mult), ['hT', 'edt_a'], ['hT'])
                        C.op('dve', lambda e: e.tensor_tensor(out=hT[:], in0=hT[:], in1=psf[3][:, 0:512], op=ALU.add), ['hT', PK(3)], ['hT'])
                        C.op('act', lambda e: e.activation(out=hTb[:], in_=hT[:], func=AF.Copy), ['hT'], ['hTb'])
                    else:
                        for b in range(4):
                            pi = 3 if b % 2 == 0 else 4
                            C.op('dve', lambda e: e.tensor_scalar(out=xddm[:], in0=xdd[:], scalar1=cc(C_BLK + b, 1), scalar2=None, op0=ALU.mult), ['xdd', 'cst'], ['xddm'])
                            C.op('pe', lambda e: e.matmul(psf[pi][:, 0:512], lhsT=b_tm[:], rhs=xddm[:], start=True, stop=True), ['b_tm', 'xddm'], [PK(pi)])
                            C.op('dve', lambda e: e.tensor_tensor(out=hTs[:, b, :].rearrange("p (h d) -> p h d", d=64), in0=hTs[:, b, :].rearrange("p (h d) -> p h d", d=64),
                                                                  in1=edt[:, b, :].unsqueeze(2).to_broadcast([128, 8, 64]), op=ALU.mult), ['hTs', 'edt_a'], ['hTs'])
                            C.op('dve', lambda e: e.tensor_tensor(out=hTs[:, b, :], in0=hTs[:, b, :], in1=psf[pi][:, 0:512], op=ALU.add), ['hTs', PK(pi)], ['hTs'])
                    if os.environ.get("KCH") == "6" and g >= int(os.environ.get("KG", "0")):
                        return stop_here()
            if os.environ.get("KCH") == "7" and g >= int(os.environ.get("KG", "0")):
                return stop_here()
            C.dma('sp', lambda e: e.dma_start(out=hTp_o[:, 512 * g:512 * (g + 1)], in_=hT[:]), ['hT'], [])
            C.dma('sp', lambda e: e.dma_start(out=hTs_o[:, :, 512 * g:512 * (g + 1)].rearrange("b n f -> n b f"), in_=hTs[:]), ['hTs'], [])
    C.barrier()
    SSMT_ALL = [('ssmT', bi) for bi in range(len(blocks))]

    if STOP <= 3:
        return stop_here()
    mrgT = sb("mrgT", [128, KC, NT], BF16)
    with ExitStack() as ph:
        wm = [sb("wm%d" % i, [128, 24, 128], BF16, ph) for i in range(2)]
        sgs = sb("sgs", [128, QB], F32, ph)

        def mload(j):
            w = wm[j % 2]; kk = ('wm', j % 2)
            wload(w[:, 0:16, :], wos_d, 0, 16, 128 * j, 128, kk)
            wload(w[:, 16:24, :], w_in_d, 0, 8, GA0 + 1024 + 128 * j, 128, kk)
        mload(0)
        for j in range(8):
            if j + 1 < 8:
                mload(j + 1)
            w = wm[j % 2]; wk = ('wm', j % 2)
            for bi, (b0, n) in enumerate(blocks):
                p1, p3 = 2 * (bi % 2), 2 * (bi % 2) + 1
                for k in range(16):
                    C.op('pe', lambda e: e.matmul(psf[p1][:, 0:n], lhsT=w[:, k, :], rhs=ssmT[:, k, b0:b0 + n], start=(k == 0), stop=(k == 15)), [wk] + SSMT_ALL, [PK(p1)])
                for k in range(8):
                    C.op('pe', lambda e: e.matmul(psf[p3][:, 0:n], lhsT=w[:, 16 + k, :], rhs=uT[:, k, b0:b0 + n], start=(k == 0), stop=(k == 7)), [wk] + UT_ALL, [PK(p3)])
                C.op('act', lambda e: e.activation(out=sgs[:, 0:n], in_=psf[p3][:, 0:n], func=AF.Sigmoid), [PK(p3)], ['sgs'])
                C.op('dve', lambda e: e.tensor_tensor(out=mrgT[:, j, b0:b0 + n], in0=psf[p1][:, 0:n], in1=sgs[:, 0:n], op=ALU.mult), [PK(p1), 'sgs'], [('mrgT', bi)])
    C.barrier()
    release("ssmT")

    if STOP <= 5:
        return stop_here()
    ph2 = ExitStack()
    dbias = sb("dbias", [128, NQB, PTL, 16], F32, ph2)
    cnew = sb("cnew", [128, 16], F32, ph2)
    sbias = sb("sbias", [128, 4, NPG, 16], F32, ph2)
    idx = sb("idx", [128, NPAGE], I32, ph2)
    with ExitStack() as ph:
        wf = sb("wf", [128, KC, 16], BF16, ph)
        wload(wf, w_in_d, 0, KC, F0, 16, 'wf')
        t1 = sb("t1", [128, NTL, 16], F32, ph); t2 = sb("t2", [128, NTL, 16], F32, ph)
        for t in range(NTL):
            pi = t % 4
            for k in range(KC):
                C.op('pe', lambda e: e.matmul(psf[pi][:, 0:16], lhsT=uT[:, k, t * 128:(t + 1) * 128], rhs=wf[:, k, :], start=(k == 0), stop=(k == KC - 1)),
                     ['wf'] + UT_ALL, [PK(pi)])
            C.op('dve', lambda e: e.tensor_tensor(out=t1[:, t, :], in0=psf[pi][:, 0:16], in1=prow[:, R_BF:R_BF + 16], op=ALU.add), [PK(pi), 'prow'], ['t1'])
        C.op('dve', lambda e: e.tensor_scalar(out=t2[:], in0=t1[:], scalar1=-1.0, scalar2=None, op0=ALU.mult), ['t1'], ['t2'])
        C.op('dve', lambda e: e.tensor_tensor(out=t2[:], in0=t2[:], in1=t1[:], op=ALU.min), ['t1', 't2'], ['t2'])
        C.op('act', lambda e: e.activation(out=t2[:], in_=t2[:], func=AF.Exp), ['t2'], ['t2'])
        C.op('act', lambda e: e.activation(out=t2[:], in_=t2[:], func=AF.Ln, bias=1.0), ['t2'], ['t2'])
        C.op('dve', lambda e: e.scalar_tensor_tensor(out=lf_tm[:], in0=t1[:], scalar=0.0, in1=t2[:], op0=ALU.min, op1=ALU.subtract), ['t1', 't2'], ['lf_tm'])
        C.dma('sp', lambda e: e.dma_start(out=lf_o.rearrange("(t p) h -> p t h", p=128), in_=lf_tm[:]), ['lf_tm'], [])

        def scan_mid(buf_a, buf_b, ka, kb_, nmid, view):
            s = 1; cur, ck, oth, ok = buf_a, ka, buf_b, kb_
            while s < nmid:
                C.op('dve', lambda e: e.tensor_tensor(out=view(oth, s, nmid), in0=view(cur, s, nmid), in1=view(cur, 0, nmid - s), op=ALU.add), [ck], [ok])
                C.op('pool', lambda e: e.tensor_copy(out=view(oth, 0, s), in_=view(cur, 0, s)), [ck], [ok])
                cur, ck, oth, ok = oth, ok, cur, ck
                s *= 2
            return cur, ck
        wps = sb("wps", [128, PTL, 16], F32, ph); pta = sb("pta", [128, PTL, 16], F32, ph); ptb = sb("ptb", [128, PTL, 16], F32, ph)
        pt0 = sb("pt0", [128, PTL, 16], F32, ph)
        lfp = lf_tm[:, 0:PTL, :]
        C.op('pe', lambda e: e.matmul(psf[0][:, 0:PTL * 16], lhsT=cc(C_TRI), rhs=lfp, start=True, stop=True), ['lf_tm', 'cst'], [PK(0)])
        C.op('pe', lambda e: e.matmul(psf[1][:, 0:PTL * 16], lhsT=cc(C_ONES), rhs=lfp, start=True, stop=True), ['lf_tm', 'cst'], [PK(1)])
        C.op('act', lambda e: e.activation(out=wps[:], in_=psf[0][:, 0:PTL * 16].rearrange("p (t h) -> p t h", h=16), func=AF.Copy), [PK(0)], ['wps'])
        C.op('act', lambda e: e.activation(out=pta[:], in_=psf[1][:, 0:PTL * 16].rearrange("p (t h) -> p t h", h=16), func=AF.Copy), [PK(1)], ['pta'])
        C.op('dve', lambda e: e.tensor_copy(out=pt0[:], in_=pta[:]), ['pta'], ['pt0'])
        inc, ik = scan_mid(pta, ptb, 'pta', 'ptb', PTL, lambda b, lo, hi: b[:, lo:hi, :])
        C.op('dve', lambda e: e.tensor_tensor(out=wps[:], in0=wps[:], in1=inc[:], op=ALU.add), ['wps', ik], ['wps'])
        C.op('dve', lambda e: e.tensor_tensor(out=wps[:], in0=wps[:], in1=pt0[:], op=ALU.subtract), ['wps', 'pt0'], ['wps'])
        for qb in range(NQB):
            te = (qb + 1) * (QB // 128) - 1
            C.op('dve', lambda e: e.tensor_tensor(out=dbias[:, qb, :, :], in0=inc[:, te:te + 1, :].to_broadcast([128, PTL, 16]), in1=wps[:], op=ALU.subtract),
                 ['wps', ik], ['dbias'])
        C.op('pe', lambda e: e.matmul(psf[2][:, 0:16], lhsT=cc(C_TRIB), rhs=lf_tm[:, PTL, :], start=True, stop=True), ['lf_tm', 'cst'], [PK(2)])
        C.op('act', lambda e: e.activation(out=cnew[:], in_=psf[2][:, 0:16], func=AF.Copy, scale=-1.0), [PK(2)], ['cnew'])
        idr = sb("idr", [128, NPAGE], I32, ph)
        C.dma('sp', lambda e: e.dma_start(out=idr[:], in_=ptab_d.partition_broadcast(128)), (), ['idr'])
        C.op('dve', lambda e: e.tensor_scalar(out=idx[:], in0=idr[:], scalar1=128.0, scalar2=cc(C_IOTA, 1), op0=ALU.mult, op1=ALU.add), ['idr', 'cst'], ['idx'])
        PP = min(128, NPAGE); NGRP = (NPAGE + 127) // 128
        idp = sb("idp", [128, NGRP], I32, ph)
        for a in range(NGRP):
            C.dma('sp', lambda e: e.dma_start(out=idp[0:PP, a:a + 1], in_=ptab_d[0:1, a * PP:(a + 1) * PP].rearrange("o p -> p o")), (), ['idp'])
        lfP = sb("lfP", [128, NGRP, 128, 16], F32, ph); wP = sb("wP", [128, NGRP, 128, 16], F32, ph)
        d1 = sb("d1", [128, NGRP, 16], F32, ph)
        for a in range(NGRP):
            C.dma('pool', lambda e: e.indirect_dma_start(out=lfP[0:PP, a, :, :].rearrange("p k h -> p (k h)"), out_offset=None, in_=lfpool2_d,
                                                       in_offset=bass.IndirectOffsetOnAxis(ap=idp[0:PP, a:a + 1], axis=0)), ['idp'], ['lfP'])
        for a in range(NGRP):
            for h in range(16):
                C.op('dve', lambda e: e.tensor_tensor_scan(out=wP[0:PP, a, :, h], data0=cc(C_ONES, 128, 0, PP), data1=lfP[0:PP, a, :, h], initial=0.0, op0=ALU.mult, op1=ALU.add),
                     ['lfP', 'cst'], ['wP'])
            C.op('pe', lambda e: e.matmul(psf[0][0:PP, 16 * a:16 * a + 16], lhsT=cc(C_PGLT, PP, 0, PP), rhs=wP[0:PP, a, 127, :], start=True, stop=True), ['wP', 'cst'], [PK(0)])
            C.op('pe', lambda e: e.matmul(psf[1][0:PP, 16 * a:16 * a + 16], lhsT=cc(C_PGSAME, PP, 0, PP), rhs=wP[0:PP, a, 127, :], start=True, stop=True), ['wP', 'cst'], [PK(1)])
        C.op('act', lambda e: e.activation(out=d1[0:PP, :, :], in_=psf[0][0:PP, 0:16 * NGRP].rearrange("p (a h) -> p a h", h=16), func=AF.Copy), [PK(0)], ['d1'])
        C.op('dve', lambda e: e.tensor_tensor(out=d1[0:PP, :, :], in0=psf[1][0:PP, 0:16 * NGRP].rearrange("p (a h) -> p a h", h=16), in1=d1[0:PP, :, :], op=ALU.subtract), [PK(1), 'd1'], ['d1'])
        for a in range(NGRP):
            C.op('dve', lambda e: e.tensor_tensor(out=wP[0:PP, a, :, :], in0=d1[0:PP, a, :].unsqueeze(1).to_broadcast([PP, 128, 16]), in1=wP[0:PP, a, :, :], op=ALU.subtract), ['d1', 'wP'], ['wP'])
        sbv = sbias[:].rearrange("k b g h -> k (b g) h")
        for a in range(NGRP):
            for hq in range(4):
                pi = 2 + (a * 4 + hq) % 2
                for hh in range(4):
                    h = 4 * hq + hh
                    C.op('pe', lambda e: e.transpose(psf[pi][:, hh * PP:(hh + 1) * PP], wP[0:PP, a, :, h], cc(C_ID, PP, 0, PP)), ['wP', 'cst'], [PK(pi)])
                C.op('act', lambda e: e.activation(out=sbv[:, a * PP:(a + 1) * PP, 4 * hq:4 * hq + 4], in_=psf[pi][:, 0:4 * PP].rearrange("k (h p) -> k p h", h=4), func=AF.Copy), [PK(pi)], ['sbias'])
    C.barrier()

    if STOP <= 6:
        return stop_here()
    attnT = sb("attnT", [128, KC, NT], BF16)
    qTs = sb("qTs", [128, KC, 128], BF16); kTs = sb("kTs", [128, KC, 128], BF16); v_s = sb("v_s", [128, 1024], BF16)
    C.op('pool', lambda e: e.memset(attnT[:, :, T:NT], 0.0), (), ['attnT'])
    with ExitStack() as ph:
        wqkv = [sb("wqkv%d" % i, [128, KC, 384], BF16, ph) for i in range(2)]
        qT = sb("qT", [128, NT], BF16, ph); kT = sb("kT", [128, NT], BF16, ph)
        vaug = sb("vaug", [128, NTL, 2, 128], BF16, ph)
        kvst = sb("kvst", [128, 2, 256], F32, ph)
        pT = [sb("pTt%d" % i, [128, QB], BF16, ph) for i in range(3)]
        rec = sb("rec", [128, QB], F32, ph)
        C.op('pool', lambda e: e.memset(vaug[:, :, :, 64:128], 1.0), (), ['vaug'])

        def qload(hp):
            w = wqkv[hp % 2]; kk = ('wqkv', hp % 2)
            wload(w[:, :, 0:128], w_in_d, 0, KC, Q0 + 128 * hp, 128, kk)
            wload(w[:, :, 128:256], w_in_d, 0, KC, K0 + 128 * hp, 128, kk)
            wload(w[:, :, 256:384], w_in_d, 0, KC, V0 + 128 * hp, 128, kk)
        qload(0)
        for hp in range(8):
            if hp + 1 < 8:
                qload(hp + 1)
            w = wqkv[hp % 2]; wk = ('wqkv', hp % 2)
            for bi, (b0, n) in enumerate(blocks):
                fm_proj(w, wk, 0, uT, [('uT', bi)], KC, 0, b0, n)
                C.op('act', lambda e: e.activation(out=qT[:, b0:b0 + n], in_=psf[0][:, 0:n], func=AF.Copy, scale=0.125), [PK(0)], ['qT'])
                fm_proj(w, wk, 128, uT, [('uT', bi)], KC, 1, b0, n)
                C.op('dve', lambda e: e.tensor_scalar(out=kT[:, b0:b0 + n], in0=psf[1][:, 0:n], scalar1=1.0, scalar2=None, op0=ALU.mult), [PK(1)], ['kT'])
            KP4 = int(os.environ.get("KP4", "9"))
            for t in range(NTL if KP4 >= 2 else 0):
                pi = 2 + t % 2
                for k in range(KC):
                    C.op('pe', lambda e: e.matmul(psf[pi][:, 0:256], lhsT=uT[:, k, t * 128:(t + 1) * 128], rhs=w[:, k, 128:384], start=(k == 0), stop=(k == KC - 1)), [wk] + UT_ALL, [PK(pi)])
                C.op('act', lambda e: e.activation(out=kvst[:, t % 2, :], in_=psf[pi][:, 0:256], func=AF.Copy), [PK(pi)], [('kvst', t % 2)])
                if not os.environ.get("KNOVAUG"):
                    C.op('act', lambda e: e.activation(out=vaug[:, t, :, 0:64], in_=psf[pi][:, 128:256].rearrange("p (e d) -> p e d", d=64), func=AF.Copy), [PK(pi)], ['vaug'])
                if KP4 >= 3:
                    C.dma('sp', lambda e: e.dma_start(out=k_o[128 * t:128 * (t + 1), 128 * hp:128 * (hp + 1)], in_=kvst[:, t % 2, 0:128]), [('kvst', t % 2)], [])
                    C.dma('sp', lambda e: e.dma_start(out=v_o[128 * t:128 * (t + 1), 128 * hp:128 * (hp + 1)], in_=kvst[:, t % 2, 128:256]), [('kvst', t % 2)], [])
            if KP4 < 4:
                continue
            C.op('pool', lambda e: e.tensor_copy(out=qTs[:, hp, :], in_=qT[:, T:NT]), ['qT'], ['qTs'])
            C.op('pool', lambda e: e.tensor_copy(out=kTs[:, hp, :], in_=kT[:, T:NT]), ['kT'], ['kTs'])
            C.op('pool', lambda e: e.tensor_copy(out=v_s[:, 128 * hp:128 * (hp + 1)].rearrange("p (e d) -> p e d", d=64), in_=vaug[:, PTL, :, 0:64]), ['vaug'], ['v_s'])
            un = 0
            for qb in range(NQB if not os.environ.get("KSKIP_ATT") else 0):
                Q0_ = qb * QB
                for e_ in range(2):
                    h = 2 * hp + e_
                    r0, r1 = 64 * e_, 64 * e_ + 64
                    acc = e_
                    nkb = (Q0_ + QB) // 128
                    units = []
                    for kb in range(nkb):
                        qlo = max(Q0_, 128 * kb)
                        units.append((kb, qlo, Q0_ + QB - qlo, 2 + un % 3, un % 3)); un += 1

                    def emit_s(u):
                        kb, qlo, N, sbk, pk = u
                        pt_ = pT[pk]; ptk = ('pT', pk)
                        C.op('pe', lambda e: e.matmul(psf[sbk][:, 0:N], lhsT=kT[r0:r1, 128 * kb:128 * kb + 128], rhs=qT[r0:r1, qlo:qlo + N], start=True, stop=True), ['kT', 'qT'], [PK(sbk)])
                        C.op('act', lambda e: e.activation(out=pt_[:, 0:N], in_=psf[sbk][:, 0:N], func=AF.Exp, bias=dbias[:, qb, kb, h:h + 1]), [PK(sbk), 'dbias'], [ptk])
                        if 128 * kb >= Q0_:
                            C.op('pool', lambda e: e.tensor_tensor(out=pt_[:, 0:128], in0=pt_[:, 0:128], in1=ccb(C_TRI), op=ALU.mult), [ptk, 'cstb'], [ptk])

                    def emit_pv(u):
                        kb, qlo, N, sbk, pk = u
                        pt_ = pT[pk]; ptk = ('pT', pk)
                        C.op('pe', lambda e: e.matmul(psf[acc][:, qlo - Q0_:QB], lhsT=vaug[:, kb, e_, :], rhs=pt_[:, 0:N], start=(kb == 0), stop=(kb == nkb - 1)), [ptk, 'vaug'], [PK(acc)])
                    SK = 2
                    for i in range(len(units) + SK):
                        if i < len(units):
                            emit_s(units[i])
                        if i >= SK:
                            emit_pv(units[i - SK])
                    C.op('dve', lambda e: e.reciprocal(out=rec[64:128, :], in_=psf[acc][64:128, 0:QB]), [PK(acc)], ['rec'])
                    C.op('dve', lambda e: e.tensor_tensor(out=attnT[r0:r1, hp, Q0_:Q0_ + QB], in0=psf[acc][0:64, 0:QB], in1=rec[64:128, :], op=ALU.mult), [PK(acc), 'rec'], [('attnT', qb)])
    C.barrier()

    if STOP <= 7:
        return stop_here()
    with ExitStack() as ph:
        kpg = [sb("kpg%d" % i, [128, KC, 128], BF16, ph) for i in range(3)]
        vpg = [sb("vpg%d" % i, [128, 1024], BF16, ph) for i in range(3)]
        qblk = sb("qblk", [128, KC, 64], BF16, ph)
        stmp = [sb("stmp%d" % i, [128, 64], F32, ph) for i in range(2)]
        pts = [sb("pts%d" % i, [128, 64], BF16, ph) for i in range(2)]
        osel = sb("osel", [64, 16, 64], F32, ph); ored = sb("ored", [64, 64], F32, ph); den = sb("den", [64, 1], F32, ph)
        on = sb("on", [64, 64], BF16, ph); otr = sb("otr", [64, 64], BF16, ph)

        def pgload(j):
            C.dma('pool', lambda e: e.indirect_dma_start(out=kpg[j % 3][:].rearrange("p c k -> p (c k)"), out_offset=None, in_=kpool_d,
                                                       in_offset=bass.IndirectOffsetOnAxis(ap=idx[:, j:j + 1], axis=0)), ['idx'], [('kpg', j % 3)])
            C.dma('pool', lambda e: e.indirect_dma_start(out=vpg[j % 3][:], out_offset=None, in_=vpool_d,
                                                       in_offset=bass.IndirectOffsetOnAxis(ap=idx[:, j:j + 1], axis=0)), ['idx'], [('vpg', j % 3)])
        pgload(0)
        if NPAGE > 1:
            pgload(1)
        for b in range(4):
            C.op('pool', lambda e: e.memset(qblk[:], 0.0), (), ['qblk'])
            for h in range(16):
                c, r0 = h // 2, 64 * (h % 2)
                C.op('dve', lambda e: e.tensor_copy(out=qblk[r0:r0 + 64, c, 4 * h:4 * h + 4], in_=qTs[r0:r0 + 64, c, 32 * b:32 * b + 4]), ['qTs'], ['qblk'])
            for g_ in range(NPG + 1):
                j = b * NPG + g_
                new = (g_ == NPG)
                if not new and j + 2 < NPAGE:
                    pgload(j + 2)
                s_ = g_ % 2
                if not new:
                    for c in range(KC):
                        C.op('pe', lambda e: e.matmul(psf[3 + s_][:, 0:64], lhsT=kpg[j % 3][:, c, :], rhs=qblk[:, c, :], start=(c == 0), stop=(c == KC - 1)), [('kpg', j % 3), 'qblk'], [PK(3 + s_)])
                    C.op('dve', lambda e: e.tensor_tensor(out=stmp[s_][:].rearrange("p (h q) -> p h q", q=4), in0=psf[3 + s_][:, 0:64].rearrange("p (h q) -> p h q", q=4),
                                                          in1=sbias[:, b, g_, :].unsqueeze(2).to_broadcast([128, 16, 4]), op=ALU.add), [PK(3 + s_), 'sbias'], [('stmp', s_)])
                    C.op('act', lambda e: e.activation(out=pts[s_][:], in_=stmp[s_][:], func=AF.Exp), [('stmp', s_)], [('pts', s_)])
                    lh = pts[s_][:, :]; vr = vpg[j % 3]; vk = ('vpg', j % 3); ones_l = ccb(C_ONES, 1)
                else:
                    for c in range(KC):
                        C.op('pe', lambda e: e.matmul(psf[3 + s_][:, 0:64], lhsT=kTs[:, c, :], rhs=qblk[:, c, :], start=(c == 0), stop=(c == KC - 1)), ['kTs', 'qblk'], [PK(3 + s_)])
                    C.op('dve', lambda e: e.tensor_tensor(out=stmp[s_][:].rearrange("p (h q) -> p h q", q=4), in0=psf[3 + s_][:, 0:64].rearrange("p (h q) -> p h q", q=4),
                                                          in1=cnew[:].unsqueeze(2).to_broadcast([128, 16, 4]), op=ALU.add), [PK(3 + s_), 'cnew'], [('stmp', s_)])
                    C.op('act', lambda e: e.activation(out=pts[s_][:], in_=stmp[s_][:], func=AF.Exp), [('stmp', s_)], [('pts', s_)])
                    C.op('dve', lambda e: e.tensor_tensor(out=pts[s_][:], in0=pts[s_][:], in1=ccb(C_CAUS + 64 * b, 64), op=ALU.mult), [('pts', s_), 'cstb'], [('pts', s_)])
                    lh = pts[s_][:, :]; vk = 'v_s'; ones_l = ccb(C_ONES, 1)
                st_, sp_ = (g_ == 0), new
                if not new:
                    C.op('pe', lambda e: e.matmul(psf[0][0:64, 0:512], lhsT=lh, rhs=vpg[j % 3][:, 0:512], start=st_, stop=sp_), [('pts', s_), vk], [PK(0)])
                    C.op('pe', lambda e: e.matmul(psf[1][0:64, 0:512], lhsT=lh, rhs=vpg[j % 3][:, 512:1024], start=st_, stop=sp_), [('pts', s_), vk], [PK(1)])
                else:
                    C.op('pe', lambda e: e.matmul(psf[0][0:64, 0:512], lhsT=lh, rhs=v_s[:, 0:512], start=st_, stop=sp_), [('pts', s_), vk], [PK(0)])
                    C.op('pe', lambda e: e.matmul(psf[1][0:64, 0:512], lhsT=lh, rhs=v_s[:, 512:1024], start=st_, stop=sp_), [('pts', s_), vk], [PK(1)])
                C.op('pe', lambda e: e.matmul(psf[2][0:64, 0:1], lhsT=lh, rhs=ones_l, start=st_, stop=sp_), [('pts', s_), 'cstb'], [PK(2)])
            for hh in range(2):
                C.op('dve', lambda e: e.tensor_tensor(out=osel[:, 8 * hh:8 * hh + 8, :], in0=psf[hh][0:64, 0:512].rearrange("p (h d) -> p h d", d=64),
                                                      in1=cc(C_BD + 8 * hh, 8, 0, 64).unsqueeze(2).to_broadcast([64, 8, 64]), op=ALU.mult), [PK(hh), 'cst'], ['osel'])
            C.op('dve', lambda e: e.tensor_reduce(out=ored[:], in_=osel[:].rearrange("p h d -> p d h"), axis=AX.X, op=ALU.add), ['osel'], ['ored'])
            C.op('dve', lambda e: e.reciprocal(out=den[:], in_=psf[2][0:64, 0:1]), [PK(2)], ['den'])
            C.op('dve', lambda e: e.tensor_scalar(out=on[:], in0=ored[:], scalar1=den[:, 0:1], scalar2=None, op0=ALU.mult), ['ored', 'den'], ['on'])
            C.op('pe', lambda e: e.transpose(psb[0][0:64, 0:64], on[:], ccb(C_ID, 64, 0, 64)), ['on', 'cstb'], [PB(0)])
            C.op('act', lambda e: e.activation(out=otr[:], in_=psb[0][0:64, 0:64], func=AF.Copy), [PB(0)], ['otr'])
            for h in range(16):
                c, r0 = h // 2, 64 * (h % 2)
                C.op('dve', lambda e: e.tensor_copy(out=attnT[r0:r0 + 64, c, T + 32 * b:T + 32 * b + 4], in_=otr[:, 4 * h:4 * h + 4]), ['otr'], ['attnT'])
    ph2.close()
    C.barrier()
    ATT_ALL = ['attnT'] + [('attnT', qb) for qb in range(NQB)]

    if STOP <= 8:
        return stop_here()
    with ExitStack() as ph:
        wm = [sb("wmb%d" % i, [128, 16, 128], BF16, ph) for i in range(2)]
        sga = sb("sga", [128, QB], F32, ph); tma = sb("tma", [128, QB], F32, ph)

        def mload2(j):
            w = wm[j % 2]; kk = ('wmb', j % 2)
            wload(w[:, 0:8, :], woa_d, 0, 8, 128 * j, 128, kk)
            wload(w[:, 8:16, :], w_in_d, 0, 8, GA0 + 128 * j, 128, kk)
        mload2(0)
        for j in range(8):
            if j + 1 < 8:
                mload2(j + 1)
            w = wm[j % 2]; wk = ('wmb', j % 2)
            for bi, (b0, n) in enumerate(blocks):
                p0, p2 = 2 * (bi % 2), 2 * (bi % 2) + 1
                for k in range(8):
                    C.op('pe', lambda e: e.matmul(psf[p0][:, 0:n], lhsT=w[:, k, :], rhs=attnT[:, k, b0:b0 + n], start=(k == 0), stop=(k == 7)), [wk] + ATT_ALL, [PK(p0)])
                for k in range(8):
                    C.op('pe', lambda e: e.matmul(psf[p2][:, 0:n], lhsT=w[:, 8 + k, :], rhs=uT[:, k, b0:b0 + n], start=(k == 0), stop=(k == 7)), [wk] + UT_ALL, [PK(p2)])
                C.op('act', lambda e: e.activation(out=sga[:, 0:n], in_=psf[p2][:, 0:n], func=AF.Sigmoid), [PK(p2)], ['sga'])
                C.op('dve', lambda e: e.tensor_tensor(out=tma[:, 0:n], in0=psf[p0][:, 0:n], in1=sga[:, 0:n], op=ALU.mult), [PK(p0), 'sga'], ['tma'])
                C.op('pool', lambda e: e.tensor_tensor(out=mrgT[:, j, b0:b0 + n], in0=tma[:, 0:n], in1=mrgT[:, j, b0:b0 + n], op=ALU.add), ['tma', ('mrgT', bi)], [('mrgT', bi)])
    C.barrier()
    release("attnT"); release("uT"); release("qTs"); release("kTs"); release("v_s")
    MRG_ALL = [('mrgT', bi) for bi in range(len(blocks))]
    xres = sb("xres", [128, KC, NT], F32)

    def gen_sweep(wd, nk, rhsT, rhs_all, wtiles, wkeyname, post):
        def ld(j):
            wload(wtiles[j % 2], wd, 0, nk, 128 * j, 128, (wkeyname, j % 2))
        ld(0)
        un = 0
        for j in range(8):
            if j + 1 < 8:
                ld(j + 1)
            for bi, (b0, n) in enumerate(blocks):
                pi = un % 4; un += 1
                fm_proj(wtiles[j % 2], (wkeyname, j % 2), 0, rhsT, rhs_all, nk, pi, b0, n)
                post(j, bi, b0, n, pi)

    with ExitStack() as ph:
        wo = [sb("wo%d" % i, [128, 8, 128], BF16, ph) for i in range(2)]
        xs2 = [sb("xs2_%d" % i, [128, QB], F32, ph) for i in range(2)]
        cnt = [0]

        def post_out(j, bi, b0, n, pi):
            s_ = cnt[0] % 2; cnt[0] += 1
            C.dma('sp', lambda e: e.dma_start(out=xs2[s_][:, 0:n], in_=xT_d[128 * j:128 * (j + 1), b0:b0 + n]), (), [('xs2', s_)])
            C.op('dve', lambda e: e.tensor_tensor(out=xres[:, j, b0:b0 + n], in0=psf[pi][:, 0:n], in1=xs2[s_][:, 0:n], op=ALU.add), [PK(pi), ('xs2', s_)], [('xres', bi)])
        gen_sweep(wout_d, 8, mrgT, MRG_ALL, wo, 'wo', post_out)
    C.barrier()
    release("mrgT")
    XR_ALL = [('xres', bi) for bi in range(len(blocks))]

    if STOP <= 9:
        return stop_here()
    uT = sb("uT", [128, KC, NT], BF16)
    u2T = uT
    rms_to_bf(lambda bi, b0, n: (xres[:, :, b0:b0 + n], ('xres', bi)), None, P_GFFN, u2T, 'uT', normst)
    C.barrier()
    with ExitStack() as ph:
        HF = 11
        hidT = sb("hidT", [128, HF, NT], BF16, ph)
        wu = [sb("wu%d" % i, [128, KC, 256], BF16, ph) for i in range(2)]
        wd_ = [sb("wd%d" % i, [128, HF, 128], BF16, ph) for i in range(2)]
        apad = sb("apad", [128, 2 + QB], F32, ph); asp = sb("asp", [128, 4, 6], F32, ph)
        ct2 = sb("ct2", [128, QB], F32, ph); sl2 = sb("sl2", [128, QB], F32, ph)
        aust = sb("aust", [128, 128], F32, ph)
        cfw = lambda j, i: pfm[:, P_CFW + j * 3 + i:P_CFW + j * 3 + i + 1]

        def uload(jj):
            w = wu[jj % 2]; kk = ('wu', jj % 2)
            wload(w[:, :, 0:128], wup_d, 0, KC, 128 * jj, 128, kk)
            wload(w[:, :, 128:256], wup_d, 0, KC, DFF + 128 * jj, 128, kk)
        uload(0)
        for half in range(2):
            C.op('pool', lambda e: e.memset(hidT[:, :, T:NT], 0.0), (), [('hidT', len(blocks) - 1)])
            for jl in range(HF):
                jj = half * HF + jl
                if jj + 1 < FC:
                    uload(jj + 1)
                w = wu[jj % 2]; wk = ('wu', jj % 2)
                C.op('pool', lambda e: e.memset(apad[:, 0:2], 0.0), (), ['apad'])
                C.dma('sp', lambda e: e.dma_start(out=asp[:, :, 0:2], in_=cf0_d[128 * jj:128 * (jj + 1), :, :]), (), ['asp'])
                for bi, (b0, n) in enumerate(blocks):
                    samp = (b0 == T)
                    fm_proj(w, wk, 0, u2T, [('uT', bi)], KC, 0, b0, n)
                    fm_proj(w, wk, 128, u2T, [('uT', bi)], KC, 1, b0, n)
                    if not samp:
                        evac(apad[:, 2:2 + n], psf[0][:, 0:n], [PK(0)], ['apad'])
                        xin = lambda i: apad[:, i:i + n]
                        cto = ct2[:, 0:n]; slo = sl2[:, 0:n]; rk = 'apad'
                        vin = psf[1][:, 0:n]
                        ho = hidT[:, jl, b0:b0 + n]
                    else:
                        evac(asp[:, :, 2:6], psf[0][:, 0:128].rearrange("p (b t) -> p b t", t=32)[:, :, 0:4], [PK(0)], ['asp'])
                        xin = lambda i: asp[:, :, i:i + 4]
                        cto = ct2[:, 0:16].rearrange("p (b t) -> p b t", t=4); slo = sl2[:, 0:16].rearrange("p (b t) -> p b t", t=4); rk = 'asp'
                        vin = psf[1][:, 0:128].rearrange("p (b t) -> p b t", t=32)[:, :, 0:4]
                        ho = hidT[:, jl, T:NT].rearrange("p (b t) -> p b t", t=32)[:, :, 0:4]
                    C.op('dve', lambda e: e.tensor_scalar(out=cto, in0=xin(0), scalar1=cfw(jj, 0), scalar2=pfm[:, P_CFB + jj:P_CFB + jj + 1], op0=ALU.mult, op1=ALU.add), [rk, 'pfm'], ['ct2'])
                    for i in range(1, 3):
                        C.op('dve', lambda e: e.scalar_tensor_tensor(out=cto, in0=xin(i), scalar=cfw(jj, i), in1=cto, op0=ALU.mult, op1=ALU.add), [rk, 'ct2', 'pfm'], ['ct2'])
                    C.op('act', lambda e: e.activation(out=slo, in_=cto, func=AF.Silu), ['ct2'], ['sl2'])
                    C.op('dve', lambda e: e.tensor_tensor(out=ho, in0=vin, in1=slo, op=ALU.mult), [PK(1), 'sl2'], [('hidT', bi)])
                    if not samp:
                        C.op('pool', lambda e: e.tensor_copy(out=apad[:, 0:2], in_=apad[:, n:n + 2]), ['apad'], ['apad'])
                    if bi >= len(blocks) - 2:
                        tcol = b0 + n - 128
                        orow = 0 if not samp else 128
                        for k in range(KC):
                            C.op('pe', lambda e: e.matmul(psf[2][:, 0:128], lhsT=u2T[:, k, tcol:tcol + 128], rhs=w[:, k, 0:128], start=(k == 0), stop=(k == KC - 1)), [wk, ('uT', bi)], [PK(2)])
                        evac(aust[:], psf[2][:, 0:128], [PK(2)], ['aust'])
                        C.dma('sp', lambda e: e.dma_start(out=aup_o[orow:orow + 128, 128 * jj:128 * (jj + 1)], in_=aust[:]), ['aust'], [])
            HID_ALL = [('hidT', bi) for bi in range(len(blocks))]

            def dload(j):
                wload(wd_[j % 2], wdn_d, half * HF * 128, HF, 128 * j, 128, ('wd', j % 2))
            dload(0)
            for j in range(8):
                if j + 1 < 8:
                    dload(j + 1)
                for bi, (b0, n) in enumerate(blocks):
                    pi = 2 + (j * len(blocks) + bi) % 3
                    fm_proj(wd_[j % 2], ('wd', j % 2), 0, hidT, HID_ALL, HF, pi, b0, n)
                    C.op('dve', lambda e: e.tensor_tensor(out=xres[:, j, b0:b0 + n], in0=xres[:, j, b0:b0 + n], in1=psf[pi][:, 0:n], op=ALU.add), [PK(pi), ('xres', bi)], [('xres', bi)])
    C.barrier()

    if STOP <= 10:
        return stop_here()
    u3T = uT
    rms_to_bf(lambda bi, b0, n: (xres[:, :, b0:b0 + n], ('xres', bi)), None, P_GPLE, u3T, 'uT', normst)
    C.barrier()
    with ExitStack() as ph:
        wpg = [sb("wpg%d" % i, [128, 10, 128], BF16, ph) for i in range(2)]
        pTb = sb("pTb", [128, 2, NT], BF16, ph)
        sg3 = sb("sg3", [128, QB], F32, ph)
        C.dma('pool', lambda e: e.dma_start(out=pTb[:], in_=pT_d.rearrange("(k p) n -> p k n", p=128)), (), ['pTb'])

        def pload(j):
            wload(wpg[j % 2][:, 0:8, :], wpg_d, 0, 8, 128 * j, 128, ('wpg', j % 2))
            wload(wpg[j % 2][:, 8:10, :], wpi_d, 0, 2, 128 * j, 128, ('wpg', j % 2))
        pload(0)
        for j in range(8):
            if j + 1 < 8:
                pload(j + 1)
            w = wpg[j % 2]; wk = ('wpg', j % 2)
            for bi, (b0, n) in enumerate(blocks):
                p0, p1 = 2 * (bi % 2), 2 * (bi % 2) + 1
                for k in range(8):
                    C.op('pe', lambda e: e.matmul(psf[p0][:, 0:n], lhsT=w[:, k, :], rhs=u3T[:, k, b0:b0 + n], start=(k == 0), stop=(k == 7)), [wk, ('uT', bi)], [PK(p0)])
                for k in range(2):
                    C.op('pe', lambda e: e.matmul(psf[p1][:, 0:n], lhsT=w[:, 8 + k, :], rhs=pTb[:, k, b0:b0 + n], start=(k == 0), stop=(k == 1)), [wk, 'pTb'], [PK(p1)])
                C.op('act', lambda e: e.activation(out=sg3[:, 0:n], in_=psf[p0][:, 0:n], func=AF.Sigmoid), [PK(p0)], ['sg3'])
                C.op('dve', lambda e: e.tensor_tensor(out=sg3[:, 0:n], in0=psf[p1][:, 0:n], in1=sg3[:, 0:n], op=ALU.mult), [PK(p1), 'sg3'], ['sg3'])
                C.op('pool', lambda e: e.tensor_tensor(out=xres[:, j, b0:b0 + n], in0=xres[:, j, b0:b0 + n], in1=sg3[:, 0:n], op=ALU.add), ['sg3', ('xres', bi)], [('xres', bi)])
    C.barrier()
    with ExitStack() as ph:
        yst = [sb("yst%d" % i, [128, KC, QB], F32, ph) for i in range(2)]
        sqf = [sb("sqf%d" % i, [128, KC, QB], BF16, ph) for i in range(2)]
        yTv = yT_o.rearrange("(c p) n -> p c n", p=128)
        for bi, (b0, n) in enumerate(blocks):
            sq = sqf[bi % 2]; rs = normst['rs'][bi % 2]; pi = 4 + bi % 2
            C.op('act', lambda e: e.activation(out=sq[:, :, 0:n], in_=xres[:, :, b0:b0 + n], func=AF.Square), [('xres', bi)], [('sq', bi % 2)])
            for c in range(KC):
                C.op('pe', lambda e: e.matmul(psf[pi][:, 0:n], lhsT=ccb(C_ONES), rhs=sq[:, c, 0:n], start=(c == 0), stop=(c == KC - 1)), [('sq', bi % 2), 'cstb'], [PK(pi)])
            C.op('act', lambda e: e.activation(out=rs[:, 0:n], in_=psf[pi][:, 0:n], func=AF.Sqrt, scale=1.0 / D, bias=normst['eps'][:, 0:1]), [PK(pi), 'epsc'], [('rs', bi % 2)])
            C.op('dve', lambda e: e.reciprocal(out=rs[:, 0:n], in_=rs[:, 0:n]), [('rs', bi % 2)], [('rs', bi % 2)])
            for c in range(KC):
                eng = 'dve'
                C.op(eng, lambda e: e.scalar_tensor_tensor(out=yst[bi % 2][:, c, 0:n], in0=xres[:, c, b0:b0 + n], scalar=pfm[:, P_GFIN + c:P_GFIN + c + 1],
                                                           in1=rs[:, 0:n], op0=ALU.mult, op1=ALU.mult), [('xres', bi), ('rs', bi % 2), 'pfm'], [('yst', bi % 2)])
            C.dma('sp', lambda e: e.dma_start(out=yTv[:, :, b0:b0 + n], in_=yst[bi % 2][:, :, 0:n]), [('yst', bi % 2)], [])
    C.final()
    es.close()
    return nc


def make_consts(NPG):
    p = np.arange(128)[:, None]; l = np.arange(128)[None, :]
    c = np.zeros((128, C_W), np.float32)
    c[:, C_ID:C_ID + 128] = (p == l)
    c[:, C_TRI:C_TRI + 128] = (p <= l)
    c[:, C_STRICT:C_STRICT + 128] = (p > l)
    c[:, C_ONES:C_ONES + 128] = 1.0
    same = (p // 32) == (l // 32)
    c[:, C_TRIB:C_TRIB + 128] = same & (p <= l)
    c[:, C_STRICTB:C_STRICTB + 128] = same & (p > l)
    c[:, C_ONESB:C_ONESB + 128] = same
    c[:, C_VALID] = (np.arange(128) % 32) < 4
    r = np.arange(64)[:, None]; hh = np.arange(16)[None, :]
    c[0:64, C_BD:C_BD + 16] = ((r // 4) == hh)
    kq = np.arange(64)[None, :] % 4
    for b in range(4):
        c[:, C_CAUS + 64 * b:C_CAUS + 64 * (b + 1)] = ((p % 32) <= kq) & ((p % 32) < 4) & ((p // 32) == b)
        c[:, C_BLK + b] = (np.arange(128) // 32) == b
        c[:, C_COLSEL + 128 * b:C_COLSEL + 128 * (b + 1)] = ((l // 32) == b)
    c[:, C_IOTA] = np.arange(128)
    samepg = (p // NPG) == (l // NPG)
    c[:, C_PGLT:C_PGLT + 128] = samepg & (p < l)
    c[:, C_PGSAME:C_PGSAME + 128] = samepg
    return c


_NC_CACHE = {}


def run(inp, T, NPG, n_cores):
    f = lambda a: np.ascontiguousarray(np.asarray(a, dtype=np.float32))
    NT = T + 128
    NPOOL = inp['cache_k'].shape[1]
    key = (T, NPG, NPOOL)
    if key not in _NC_CACHE:
        _NC_CACHE[key] = build(T, NPG, NPOOL)
    nc = _NC_CACHE[key]
    xp = f(inp['x_prompt']); xs = f(inp['x_sample']); pp = f(inp['p_prompt'])[0]; ps_ = f(inp['p_sample'])[0]
    ck = f(inp['cache_k'])[0]; cv = f(inp['cache_v'])[0]; clf = f(inp['cache_logf'])[0]
    kpool = np.ascontiguousarray(ck.reshape(NPOOL, 128, 8, 128).transpose(0, 3, 2, 1)).reshape(NPOOL * 128, 1024)
    vpool = cv.reshape(NPOOL * 128, 1024)
    lfpool2 = clf.reshape(NPOOL, 2048)
    fm = lambda v: f(v).reshape(-1, 128).T
    pfm = np.zeros((128, P_W), np.float32)
    pfm[:, P_GMIX:P_GMIX + 8] = fm(inp['norm_mix'][0]); pfm[:, P_GFFN:P_GFFN + 8] = fm(inp['norm_ffn'][0])
    pfm[:, P_GPLE:P_GPLE + 8] = fm(inp['norm_ple'][0]); pfm[:, P_GFIN:P_GFIN + 8] = fm(inp['norm_final'])
    pfm[:, P_CSW:P_CSW + 96] = f(inp['conv_ssm_w'][0]).reshape(4, 24, 128).transpose(2, 1, 0).reshape(128, 96)
    pfm[:, P_CSB:P_CSB + 24] = fm(inp['conv_ssm_b'][0])
    pfm[:, P_CFW:P_CFW + 66] = f(inp['conv_ffn_w'][0]).reshape(3, 22, 128).transpose(2, 1, 0).reshape(128, 66)
    pfm[:, P_CFB:P_CFB + 22] = fm(inp['conv_ffn_b'][0])
    prow = np.concatenate([f(inp['b_f'][0]), f(inp['dt_bias'][0]), f(inp['a_log'][0]), f(inp['d_skip'][0]), f(inp['norm_ssm'][0])])[None, :]
    consts = make_consts(NPG)
    shared = {'kpool': kpool, 'vpool': vpool, 'lfpool2': lfpool2, 'consts': consts, 'pvec_fm': pfm, 'pvec_row': np.ascontiguousarray(prow),
              'w_in': f(inp['w_in'])[0], 'w_o_attn': f(inp['w_o_attn'])[0], 'w_o_ssm': f(inp['w_o_ssm'])[0], 'w_out': f(inp['w_out'])[0],
              'w_up': f(inp['w_up'])[0], 'w_down': f(inp['w_down'])[0], 'w_ple_gate': f(inp['w_ple_gate'])[0], 'w_ple_in': f(inp['w_ple_in'])[0]}
    sst = f(inp['state_ssm'])[0]; scs = f(inp['state_conv_ssm'])[0]; scf = f(inp['state_conv_ffn'])[0]
    ptab = np.asarray(inp['page_table']).astype(np.int32)
    in_maps = []
    for i in range(n_cores):
        xT = np.zeros((D, NT), np.float32); pT = np.zeros((DPLE, NT), np.float32)
        xT[:, :T] = xp[i].T; pT[:, :T] = pp[i].T
        for b in range(4):
            xT[:, T + 32 * b:T + 32 * b + 4] = xs[4 * i + b].T
            pT[:, T + 32 * b:T + 32 * b + 4] = ps_[4 * i + b].T
        m = dict(shared)
        m['xT'] = xT; m['pT'] = pT
        m['ptab'] = np.ascontiguousarray(ptab[4 * i:4 * i + 4].reshape(1, -1))
        m['hT0'] = np.ascontiguousarray(sst[4 * i:4 * i + 4].transpose(0, 3, 1, 2).reshape(4, 128, DI))
        m['cs0'] = np.ascontiguousarray(scs[4 * i:4 * i + 4].transpose(2, 0, 1))
        m['cf0'] = np.ascontiguousarray(scf[4 * i:4 * i + 4].transpose(2, 0, 1))
        in_maps.append(m)
    res = run_bass_kernel_spmd(nc, in_maps, core_ids=list(range(n_cores)))
    R = res.results
    B = n_cores; DB = 4 * n_cores
    y_p = np.zeros((B, T, D), np.float32); y_s = np.zeros((DB, 4, D), np.float32)
    k_p = np.zeros((1, B, T, HA, DH), np.float32); v_p = np.zeros_like(k_p); lf_p = np.zeros((1, B, T, HA), np.float32)
    k_s = np.zeros((1, DB, 4, HA, DH), np.float32); v_s = np.zeros_like(k_s); lf_s = np.zeros((1, DB, 4, HA), np.float32)
    h_p = np.zeros((1, B, HS, 64, NS), np.float32); h_s = np.zeros((1, DB, HS, 64, NS), np.float32)
    cs_p = np.zeros((1, B, 3, CD), np.float32); cs_s = np.zeros((1, DB, 3, CD), np.float32)
    cf_p = np.zeros((1, B, 2, DFF), np.float32); cf_s = np.zeros((1, DB, 2, DFF), np.float32)
    for i in range(n_cores):
        r = R[i]
        yT = r['yT']; y_p[i] = yT[:, :T].T
        k_p[0, i] = r['k_tm'][:T].reshape(T, HA, DH); v_p[0, i] = r['v_tm'][:T].reshape(T, HA, DH); lf_p[0, i] = r['logf'][:T]
        h_p[0, i] = r['hT_p'].reshape(128, HS, 64).transpose(1, 2, 0)
        cs_p[0, i] = r['xbc_tm'][125:128]; cf_p[0, i] = r['aup_tm'][126:128]
        for b in range(4):
            s = 4 * i + b; c0 = T + 32 * b
            y_s[s] = yT[:, c0:c0 + 4].T
            k_s[0, s] = r['k_tm'][c0:c0 + 4].reshape(4, HA, DH); v_s[0, s] = r['v_tm'][c0:c0 + 4].reshape(4, HA, DH); lf_s[0, s] = r['logf'][c0:c0 + 4]
            h_s[0, s] = r['hT_s'][b].reshape(128, HS, 64).transpose(1, 2, 0)
            cs_s[0, s] = r['xbc_tm'][128 + 32 * b + 1:128 + 32 * b + 4]; cf_s[0, s] = r['aup_tm'][128 + 32 * b + 2:128 + 32 * b + 4]
    return (y_p, y_s, k_p, v_p, lf_p, h_p, cs_p, cf_p, k_s, v_s, lf_s, h_s, cs_s, cf_s)


def kernel(**inputs):
    T = inputs['x_prompt'].shape[1]
    NPG = inputs['page_table'].shape[1]
    return run(inputs, T, NPG, 8)
```

```python
import os
import numpy as np
from contextlib import ExitStack
import concourse.bass as bass
import concourse.mybir as mybir
from concourse.bass_utils import run_bass_kernel_spmd

F32 = mybir.dt.float32
BF16 = mybir.dt.bfloat16
I32 = mybir.dt.int32
ALU = mybir.AluOpType
AF = mybir.ActivationFunctionType
AX = mybir.AxisListType

D = 1024; KC = 8; HA = 16; DH = 64; DI = 2048; HS = 32; G = 4; NS = 128
CD = 3072; DFF = 2816; FC = 22; DPLE = 256
Q0 = 0; K0 = 1024; V0 = 2048; F0 = 3072; Z0 = 3088; X0 = 5136; B0 = X0 + 2048; C0 = B0 + 512
DT0 = 8208; GA0 = 8240; DIN = 10288
EPS = 1e-6
C_ID, C_TRI, C_ONES, C_STRICT, C_TRIB, C_STRICTB, C_ONESB = [i * 128 for i in range(7)]
C_VALID = 7 * 128
C_BD = C_VALID + 1
C_BLK = C_BD + 16
C_IOTA = C_BLK + 4
C_PGLT = C_IOTA + 1
C_PGSAME = C_PGLT + 128
C_F32W = C_PGSAME + 128
C_CAUS = C_F32W
C_COLSEL = C_CAUS + 256
C_W = C_COLSEL + 512
C_BFW = 384 + (C_W - C_F32W)
P_GMIX, P_GFFN, P_GPLE, P_GFIN = 0, 8, 16, 24
P_CSW = 32
P_CSB = P_CSW + 96
P_CFW = P_CSB + 24
P_CFB = P_CFW + 66
P_W = P_CFB + 22
R_BF, R_DTB, R_ALOG, R_DSK, R_NSSM = 0, 16, 48, 80, 112
R_W = 112 + 2048


class Ctx:
    def __init__(self, nc, es):
        self.nc = nc
        self.E = {'pe': nc.tensor, 'dve': nc.vector, 'act': nc.scalar, 'pool': nc.gpsimd, 'sp': nc.sync}
        self.S = {}
        self.cnt = {}
        for e in self.E:
            self.S[e] = es.enter_context(nc.semaphore("s_" + e))
            self.cnt[e] = 0
        self.ring = {}
        for q in ('sp', 'pool'):
            self.ring[q] = []
            for i in range(8):
                k = "d_%s%d" % (q, i)
                self.S[k] = es.enter_context(nc.semaphore(k))
                self.cnt[k] = 0
                self.ring[q].append(k)
        self.rr = {'sp': 0, 'pool': 0}
        self.lastw = {}
        self.readers = {}
        self.waited = {e: {} for e in self.E}

    def _wait(self, eng, evs):
        for (k, v) in evs:
            if k == eng and eng == 'pe':
                continue
            if self.waited[eng].get(k, 0) >= v:
                continue
            self.E[eng].wait_ge(self.S[k], v)
            self.waited[eng][k] = v

    def _deps(self, reads, writes):
        evs = []
        for r in reads:
            if r in self.lastw:
                evs.append(self.lastw[r])
        for w in writes:
            if w in self.lastw:
                evs.append(self.lastw[w])
            evs.extend(self.readers.get(w, {}).items())
        return evs

    def _record(self, ev, reads, writes):
        for r in reads:
            d = self.readers.setdefault(r, {})
            if d.get(ev[0], 0) < ev[1]:
                d[ev[0]] = ev[1]
        for w in writes:
            self.lastw[w] = ev
            self.readers[w] = {}

    def op(self, eng, fn, reads=(), writes=()):
        self._wait(eng, self._deps(reads, writes))
        ins = fn(self.E[eng])
        self.cnt[eng] += 1
        ins.then_inc(self.S[eng], 1)
        self._record((eng, self.cnt[eng]), reads, writes)

    def dma(self, q, fn, reads=(), writes=()):
        k = self.ring[q][self.rr[q] % 8]
        self.rr[q] += 1
        evs = self._deps(reads, writes)
        if self.cnt[k] > 0:
            evs.append((k, self.cnt[k]))
        self._wait(q, evs)
        ins = fn(self.E[q])
        self.cnt[k] += 16
        ins.then_inc(self.S[k], 16)
        self._record((k, self.cnt[k]), reads, writes)

    def barrier(self):
        evs = [(k, v) for k, v in self.cnt.items() if v > 0]
        for e in self.E:
            self._wait(e, evs)

    def final(self):
        evs = [(k, v) for k, v in self.cnt.items() if v > 0]
        self._wait('sp', evs)


def build(T, NPG, NPOOL):
    NT = T + 128
    NTL = NT // 128
    PTL = T // 128
    QB = min(512, T)
    NQB = T // QB
    blocks = []
    c0 = 0
    while c0 < T:
        blocks.append((c0, QB)); c0 += QB
    blocks.append((T, 128))
    NPAGE = 4 * NPG

    nc = bass.Bass("TRN2", target_bir_lowering=False)
    di = lambda n, s, dt=F32: nc.dram_tensor(n, s, dt, kind="ExternalInput").ap()
    do = lambda n, s, dt=F32: nc.dram_tensor(n, s, dt, kind="ExternalOutput").ap()
    xT_d = di("xT", [D, NT]); pT_d = di("pT", [DPLE, NT])
    kpool_d = di("kpool", [NPOOL * 128, 1024]); vpool_d = di("vpool", [NPOOL * 128, 1024])
    lfpool2_d = di("lfpool2", [NPOOL, 2048])
    ptab_d = di("ptab", [1, NPAGE], I32)
    hT0_d = di("hT0", [4, 128, DI]); cs0_d = di("cs0", [CD, 4, 3]); cf0_d = di("cf0", [DFF, 4, 2])
    consts_d = di("consts", [128, C_W]); pfm_d = di("pvec_fm", [128, P_W]); prow_d = di("pvec_row", [1, R_W])
    w_in_d = di("w_in", [D, DIN]); woa_d = di("w_o_attn", [D, D]); wos_d = di("w_o_ssm", [DI, D])
    wout_d = di("w_out", [D, D]); wup_d = di("w_up", [D, 2 * DFF]); wdn_d = di("w_down", [DFF, D])
    wpg_d = di("w_ple_gate", [D, D]); wpi_d = di("w_ple_in", [DPLE, D])
    yT_o = do("yT", [D, NT]); k_o = do("k_tm", [NT, 1024]); v_o = do("v_tm", [NT, 1024]); lf_o = do("logf", [NT, 16])
    hTp_o = do("hT_p", [128, DI]); hTs_o = do("hT_s", [4, 128, DI])
    xbc_o = do("xbc_tm", [256, CD]); aup_o = do("aup_tm", [256, DFF])

    es = ExitStack()
    C = Ctx(nc, es)
    STOP = int(os.environ.get("KSTOP", "99"))

    def stop_here():
        C.final()
        es.close()
        return nc
    ARENA_W = 51700
    big = es.enter_context(nc.sbuf_tensor("arena", [128, ARENA_W], F32))
    free_l = [[0, ARENA_W]]
    used = {}

    def release(name):
        o, n4 = used.pop(name)
        free_l.append([o, n4]); free_l.sort()
        i = 0
        while i + 1 < len(free_l):
            if free_l[i][0] + free_l[i][1] == free_l[i + 1][0]:
                free_l[i][1] += free_l[i + 1][1]; del free_l[i + 1]
            else:
                i += 1

    def sb(name, shape, dt=F32, st=None):
        isz = 2 if dt == BF16 else 4
        per = 1
        for d_ in shape[1:]:
            per *= d_
        n4 = (per * isz + 3) // 4
        n4 = (n4 + 15) // 16 * 16
        for f_ in free_l:
            if f_[1] >= n4:
                o = f_[0]; f_[0] += n4; f_[1] -= n4
                break
        else:
            raise RuntimeError("arena full allocating %s (%d B); used=%s" % (name, n4 * 4, {k: v[1] * 4 for k, v in used.items()}))
        assert name not in used, name
        used[name] = (o, n4)
        ap = big[0:shape[0], o:o + n4]
        if dt != F32:
            ap = ap.bitcast(dt)
        ap = ap[:, 0:per]
        if len(shape) == 3:
            ap = ap.rearrange("p (a b) -> p a b", a=shape[1])
        elif len(shape) == 4:
            ap = ap.rearrange("p (a b c) -> p a b c", a=shape[1], b=shape[2])
        if st is not None:
            st.callback(release, name)
        return ap
    psf = [es.enter_context(nc.psum_tensor("psf%d" % i, [128, 512], F32)) for i in range(6)]
    psb = [es.enter_context(nc.psum_tensor("psb%d" % i, [128, 1024], BF16)) for i in range(2)]
    PK = lambda i: ('psf', i)
    PB = lambda i: ('psb', i)
    evt = [0]

    def evac(out, in_, reads, writes, func=AF.Copy, scale=1.0):
        evt[0] += 1
        if evt[0] % 2 == 0 or func != AF.Copy or scale != 1.0 or len(out.shape) > 2 or len(in_.shape) > 2:
            C.op('act', lambda e: e.activation(out=out, in_=in_, func=func, scale=scale), reads, writes)
        else:
            C.op('dve', lambda e: e.tensor_scalar(out=out, in0=in_, scalar1=1.0, scalar2=None, op0=ALU.mult), reads, writes)

    cst = sb("cst", [128, C_F32W]); cstb = sb("cstb", [128, C_BFW], BF16)
    pfm = sb("pfm", [128, P_W]); prow = sb("prow", [128, R_W])
    abc = sb("abc", [128, HS])
    uT = sb("uT", [128, KC, NT], BF16)
    lf_tm = sb("lf_tm", [128, NTL, 16])
    C.dma('sp', lambda e: e.dma_start(out=cst[:], in_=consts_d[:, 0:C_F32W]), (), ['cst'])
    C.dma('pool', lambda e: e.dma_start(out=cstb[:, 0:384], in_=consts_d[:, 0:384]), (), ['cstb'])
    C.dma('pool', lambda e: e.dma_start(out=cstb[:, 384:C_BFW], in_=consts_d[:, C_F32W:C_W]), (), ['cstb'])
    C.dma('sp', lambda e: e.dma_start(out=pfm[:], in_=pfm_d), (), ['pfm'])
    C.dma('sp', lambda e: e.dma_start(out=prow[:], in_=prow_d.partition_broadcast(128)), (), ['prow'])
    C.op('act', lambda e: e.activation(out=abc[:], in_=prow[:, R_ALOG:R_ALOG + HS], func=AF.Exp), ['prow'], ['abc'])
    C.op('dve', lambda e: e.tensor_scalar(out=abc[:], in0=abc[:], scalar1=-1.0, scalar2=None, op0=ALU.mult), ['abc'], ['abc'])
    cc = lambda off, w=128, p0=0, p1=128: cst[p0:p1, off:off + w]
    ccb = lambda off, w=128, p0=0, p1=128: cstb[p0:p1, (off if off < 384 else off - C_F32W + 384):(off if off < 384 else off - C_F32W + 384) + w]

    def rms_to_bf(src, src_key, gcol, dst, dst_key, st):
        sqs = [sb("sq%d" % i, [128, KC, QB], BF16) for i in range(2)]
        for bi, (b0, n) in enumerate(blocks):
            xin = src(bi, b0, n)
            sq = sqs[bi % 2]
            C.op('act', lambda e: e.activation(out=sq[:, :, 0:n], in_=xin[0], func=AF.Square), [xin[1]], [('sq', bi % 2)])
            pi = 4 + bi % 2
            for c in range(KC):
                C.op('pe', lambda e: e.matmul(psf[pi][:, 0:n], lhsT=ccb(C_ONES), rhs=sq[:, c, 0:n], start=(c == 0), stop=(c == KC - 1)),
                     [('sq', bi % 2), 'cstb'], [PK(pi)])
            rs = st['rs'][bi % 2]
            C.op('act', lambda e: e.activation(out=rs[:, 0:n], in_=psf[pi][:, 0:n], func=AF.Sqrt, scale=1.0 / D, bias=st['eps'][:, 0:1]),
                 [PK(pi), 'epsc'], [('rs', bi % 2)])
            C.op('dve', lambda e: e.reciprocal(out=rs[:, 0:n], in_=rs[:, 0:n]), [('rs', bi % 2)], [('rs', bi % 2)])
            for c in range(KC):
                eng = 'dve'
                C.op(eng, lambda e: e.scalar_tensor_tensor(out=dst[:, c, b0:b0 + n], in0=xin[0][:, c, :], scalar=pfm[:, gcol + c:gcol + c + 1],
                                                           in1=rs[:, 0:n], op0=ALU.mult, op1=ALU.mult),
                     [xin[1], ('rs', bi % 2), 'pfm'], [(dst_key, bi)])
        C.barrier()
        release("sq0"); release("sq1")

    normst = {'rs': [sb("rs%d" % i, [128, QB]) for i in range(2)], 'eps': sb("epsc", [128, 1])}
    C.op('pool', lambda e: e.memset(normst['eps'][:], EPS), (), ['epsc'])
    xTv = xT_d.rearrange("(c p) n -> p c n", p=128)
    with ExitStack() as ph:
        xst = [sb("xst%d" % i, [128, KC, QB], F32, ph) for i in range(2)]

        def src_x(bi, b0, n):
            C.dma('sp', lambda e: e.dma_start(out=xst[bi % 2][:, :, 0:n], in_=xTv[:, :, b0:b0 + n]), (), [('xst', bi % 2)])
            return (xst[bi % 2][:, :, 0:n], ('xst', bi % 2))
        rms_to_bf(src_x, None, P_GMIX, uT, 'uT', normst)
    C.barrier()
    UT_ALL = [('uT', bi) for bi in range(len(blocks))]

    def wload(dst, wd, r0, nk, c0_, ncol, key):
        C.dma('pool', lambda e: e.dma_start(out=dst[:, 0:nk, 0:ncol],
                                            in_=wd[r0:r0 + nk * 128, c0_:c0_ + ncol].rearrange("(k p) n -> p k n", p=128)), (), [key])

    def fm_proj(w, wkey, wc0, rhsT, rhs_keys, nk, pbank, b0, n):
        for k in range(nk):
            C.op('pe', lambda e: e.matmul(psf[pbank][:, 0:n], lhsT=w[:, k, wc0:wc0 + 128], rhs=rhsT[:, k, b0:b0 + n],
                                          start=(k == 0), stop=(k == nk - 1)), [wkey] + rhs_keys, [PK(pbank)])

    if STOP <= 1:
        return stop_here()
    ssmT = sb("ssmT", [128, 16, NT], BF16)
    with ExitStack() as ph:
        wg = [sb("wg0", [128, KC, 1288], BF16, ph)] * 2
        xpad = sb("xpad", [128, 2, 3 + QB], F32, ph)
        halo = sb("halo", [128, 6, 3], F32, ph)
        xsp = sb("xsp", [128, 6, 4, 7], F32, ph)
        ctmp = [sb("ctmp%d" % i, [128, QB], F32, ph) for i in range(2)]
        xcT = sb("xcT", [128, 6, QB], BF16, ph)
        xs_tm = sb("xs_tm", [128, 512], BF16, ph); b_tm = sb("b_tm", [128, 128], BF16, ph)
        zs = sb("zs", [128, 512], F32, ph)
        rmat = sb("rmat", [128, 8, 128], F32, ph)
        emat = sb("emat", [128, 8, 128], BF16, ph); mT = sb("mT", [128, 8, 128], BF16, ph)
        cbm = sb("cbm", [128, 128], BF16, ph)
        xdt = sb("xdt", [128, 512], BF16, ph); xdd = sb("xdd", [128, 512], BF16, ph)
        yv = sb("yv", [128, 512], F32, ph); y2 = sb("y2", [128, 512], F32, ph); ysq = y2
        ss = sb("ss", [128, 1], F32, ph)
        ssm_tm = sb("ssm_tm", [128, 512], BF16, ph)
        hT = sb("hT", [128, 512], F32, ph); hTb = sb("hTb", [128, 512], BF16, ph)
        hTs = sb("hTs", [128, 4, 512], F32, ph); hTsb = sb("hTsb", [128, 4, 512], BF16, ph)
        dam = sb("dam", [128, 4, 8], F32, ph)
        raw_a = sb("raw_a", [128, NTL, 8], F32, ph); sp_a = sb("sp_a", [128, NTL, 8], F32, ph)
        dtt_a = sb("dtt_a", [128, NTL, 8], F32, ph); da_a = sb("da_a", [128, NTL, 8], F32, ph)
        ecs_a = sb("ecs_a", [128, NTL, 8], F32, ph); dec_a = sb("dec_a", [128, NTL, 8], F32, ph)
        edt_a = sb("edt_a", [128, NTL + 3, 8], F32, ph); dtw_a = sb("dtw_a", [128, NTL, 8], F32, ph)
        ctm = sb("ctm", [128, 4, 128], BF16, ph); xddm = sb("xddm", [128, 512], BF16, ph)
        xbst = rmat.rearrange("p h l -> p (h l)")[:, 0:768]

        def gload(g):
            w = wg[0]; kk = ('wg', 0)
            wload(w[:, :, 0:512], w_in_d, 0, KC, X0 + 512 * g, 512, kk)
            wload(w[:, :, 512:640], w_in_d, 0, KC, B0 + 128 * g, 128, kk)
            wload(w[:, :, 640:768], w_in_d, 0, KC, C0 + 128 * g, 128, kk)
            wload(w[:, :, 768:1280], w_in_d, 0, KC, Z0 + 512 * g, 512, kk)
            wload(w[:, :, 1280:1288], w_in_d, 0, KC, DT0 + 8 * g, 8, kk)
        csw = lambda j, i: pfm[:, P_CSW + j * 4 + i:P_CSW + j * 4 + i + 1]
        for g in range(G):
            gload(g)
            w = wg[0]; wk = ('wg', 0)
            gj = [4 * g + q for q in range(4)] + [16 + g, 20 + g]
            C.op('pool', lambda e: e.memset(halo[:], 0.0), (), ['halo'])
            C.op('pool', lambda e: e.memset(hT[:], 0.0), (), ['hT'])
            C.op('pool', lambda e: e.memset(hTb[:], 0.0), (), ['hTb'])
            C.dma('sp', lambda e: e.dma_start(out=hTs[:], in_=hT0_d[:, :, 512 * g:512 * (g + 1)].rearrange("b n f -> n b f")), (), ['hTs'])
            C.op('act', lambda e: e.activation(out=hTsb[:], in_=hTs[:], func=AF.Copy), ['hTs'], ['hTsb'])
            for q, j in enumerate(gj):
                C.dma('sp', lambda e: e.dma_start(out=xsp[:, q, :, 0:3], in_=cs0_d[j * 128:(j + 1) * 128, :, :]), (), ['xsp'])
            for t in range(NTL):
                pi = t % 2
                for k in range(KC):
                    C.op('pe', lambda e: e.matmul(psf[pi][:, 0:8], lhsT=uT[:, k, t * 128:(t + 1) * 128], rhs=w[:, k, 1280:1288], start=(k == 0), stop=(k == KC - 1)), [wk] + UT_ALL, [PK(pi)])
                C.op('dve', lambda e: e.tensor_tensor(out=raw_a[:, t, :], in0=psf[pi][:, 0:8], in1=prow[:, R_DTB + 8 * g:R_DTB + 8 * g + 8], op=ALU.add), [PK(pi), 'prow'], ['raw_a'])
            C.op('dve', lambda e: e.tensor_scalar(out=sp_a[:], in0=raw_a[:], scalar1=-1.0, scalar2=None, op0=ALU.mult), ['raw_a'], ['sp_a'])
            C.op('dve', lambda e: e.tensor_tensor(out=sp_a[:], in0=sp_a[:], in1=raw_a[:], op=ALU.min), ['raw_a', 'sp_a'], ['sp_a'])
            C.op('act', lambda e: e.activation(out=sp_a[:], in_=sp_a[:], func=AF.Exp), ['sp_a'], ['sp_a'])
            C.op('act', lambda e: e.activation(out=sp_a[:], in_=sp_a[:], func=AF.Ln, bias=1.0), ['sp_a'], ['sp_a'])
            C.op('dve', lambda e: e.scalar_tensor_tensor(out=dtt_a[:], in0=raw_a[:], scalar=0.0, in1=sp_a[:], op0=ALU.max, op1=ALU.add), ['raw_a', 'sp_a'], ['dtt_a'])
            C.op('dve', lambda e: e.tensor_tensor(out=da_a[:], in0=dtt_a[:], in1=abc[:, 8 * g:8 * g + 8].unsqueeze(1).to_broadcast([128, NTL, 8]), op=ALU.mult), ['dtt_a', 'abc'], ['da_a'])
            if os.environ.get("KPRE") == "0" and g >= int(os.environ.get("KG", "0")):
                return stop_here()
            C.op('dve', lambda e: e.tensor_scalar(out=da_a[:, PTL, :], in0=da_a[:, PTL, :], scalar1=cc(C_VALID, 1), scalar2=None, op0=ALU.mult), ['da_a', 'cst'], ['da_a'])
            C.op('dve', lambda e: e.tensor_tensor(out=dam[:], in0=da_a[:, PTL, :].unsqueeze(1).to_broadcast([128, 4, 8]), in1=cc(C_BLK, 4).unsqueeze(2).to_broadcast([128, 4, 8]), op=ALU.mult),
                 ['da_a', 'cst'], ['dam'])
            for t in range(NTL):
                sm_ = (t == PTL)
                TR_, ST_ = (C_TRIB, C_STRICTB) if sm_ else (C_TRI, C_STRICT)
                C.op('pe', lambda e: e.matmul(psf[2][:, 24 * t:24 * t + 8], lhsT=cc(TR_), rhs=da_a[:, t, :], start=True, stop=True), ['da_a', 'cst'], [PK(2)])
                C.op('pe', lambda e: e.matmul(psf[2][:, 24 * t + 8:24 * t + 16], lhsT=cc(ST_), rhs=da_a[:, t, :], start=True, stop=True), ['da_a', 'cst'], [PK(2)])
                if not sm_:
                    C.op('pe', lambda e: e.matmul(psf[2][:, 24 * t + 16:24 * t + 24], lhsT=cc(C_ONES), rhs=da_a[:, t, :], start=True, stop=True), ['da_a', 'cst'], [PK(2)])
                else:
                    C.op('pe', lambda e: e.matmul(psf[2][:, 24 * t + 16:24 * t + 48], lhsT=cc(C_ONES), rhs=dam[:].rearrange("p b h -> p (b h)"), start=True, stop=True), ['dam', 'cst'], [PK(2)])
            pv_ = psf[2][:, 0:24 * NTL].rearrange("p (t c) -> p t c", c=24)
            if os.environ.get("KA3"):
                for t in range(NTL):
                    C.op('act', lambda e: e.activation(out=ecs_a[:, t, :], in_=psf[2][:, 24 * t:24 * t + 8], func=AF.Exp), [PK(2)], ['ecs_a'])
                    C.op('act', lambda e: e.activation(out=dec_a[:, t, :], in_=psf[2][:, 24 * t + 8:24 * t + 16], func=AF.Exp), [PK(2)], ['dec_a'])
                    if t < PTL:
                        C.op('act', lambda e: e.activation(out=edt_a[:, t, :], in_=psf[2][:, 24 * t + 16:24 * t + 24], func=AF.Exp), [PK(2)], ['edt_a'])
            else:
                C.op('act', lambda e: e.activation(out=ecs_a[:], in_=pv_[:, :, 0:8], func=AF.Exp), [PK(2)], ['ecs_a'])
                C.op('act', lambda e: e.activation(out=dec_a[:], in_=pv_[:, :, 8:16], func=AF.Exp), [PK(2)], ['dec_a'])
                C.op('act', lambda e: e.activation(out=edt_a[:, 0:PTL, :], in_=pv_[:, 0:PTL, 16:24], func=AF.Exp), [PK(2)], ['edt_a'])
            C.op('act', lambda e: e.activation(out=edt_a[:, PTL:PTL + 4, :], in_=psf[2][:, 24 * PTL + 16:24 * PTL + 48].rearrange("p (b h) -> p b h", h=8), func=AF.Exp), [PK(2)], ['edt_a'])
            if os.environ.get("KPRE") == "1":
                return stop_here()
            C.op('pool', lambda e: e.tensor_tensor(out=dtw_a[:], in0=dtt_a[:], in1=dec_a[:], op=ALU.mult), ['dtt_a', 'dec_a'], ['dtw_a'])
            if os.environ.get("KPRE") == "2" and g >= int(os.environ.get("KG", "0")):
                return stop_here()
            for bi, (b0, n) in enumerate(blocks):
                if os.environ.get("KB") == "3" and g == 1 and bi == 1:
                    return stop_here()
                samp = (b0 == T)
                if samp:
                    C.op('pool', lambda e: e.memset(xcT[:, :, 0:128], 0.0), (), ['xcT'])
                for q, j in enumerate(gj):
                    pi = q % 4
                    xp_ = xpad[:, q % 2, :]; xk = ('xpad', q % 2)
                    fm_proj(w, wk, q * 128, uT, [('uT', bi)], KC, pi, b0, n)
                    if not samp:
                        C.op('pool', lambda e: e.tensor_copy(out=xp_[:, 0:3], in_=halo[:, q, :]), ['halo'], [xk])
                        evac(xp_[:, 3:3 + n], psf[pi][:, 0:n], [PK(pi)], [xk])
                        C.op('pool', lambda e: e.tensor_copy(out=halo[:, q, :], in_=xp_[:, n:n + 3]), [xk], ['halo'])
                    else:
                        evac(xsp[:, q, :, 3:7], psf[pi][:, 0:128].rearrange("p (b t) -> p b t", t=32)[:, :, 0:4], [PK(pi)], ['xsp'])
                    ct = ctmp[q % 2]; ck = ('ctmp', q % 2)
                    if not samp:
                        xin = lambda i: xp_[:, i:i + n]
                        cto = ct[:, 0:n]
                        xco = xcT[:, q, 0:n]
                    else:
                        xin = lambda i: xsp[:, q, :, i:i + 4]
                        cto = ct[:, 0:16].rearrange("p (b t) -> p b t", t=4)
                        xco = xcT[:, q, 0:128].rearrange("p (b t) -> p b t", t=32)[:, :, 0:4]
                    rk = 'xsp' if samp else xk
                    eng = 'dve'
                    C.op(eng, lambda e: e.tensor_scalar(out=cto, in0=xin(0), scalar1=csw(j, 0), scalar2=pfm[:, P_CSB + j:P_CSB + j + 1], op0=ALU.mult, op1=ALU.add),
                         [rk, 'pfm'], [ck])
                    for i in range(1, 4):
                        C.op(eng, lambda e: e.scalar_tensor_tensor(out=cto, in0=xin(i), scalar=csw(j, i), in1=cto, op0=ALU.mult, op1=ALU.add), [rk, ck, 'pfm'], [ck])
                    C.op('act', lambda e: e.activation(out=xco, in_=cto, func=AF.Silu), [ck], ['xcT'])
                if bi >= len(blocks) - 2:
                    tcol = b0 + n - 128
                    orow = 0 if not samp else 128
                    for k in range(KC):
                        C.op('pe', lambda e: e.matmul(psf[4][:, 0:512], lhsT=uT[:, k, tcol:tcol + 128], rhs=w[:, k, 0:512], start=(k == 0), stop=(k == KC - 1)), [wk, ('uT', bi)], [PK(4)])
                    for k in range(KC):
                        C.op('pe', lambda e: e.matmul(psf[5][:, 0:256], lhsT=uT[:, k, tcol:tcol + 128], rhs=w[:, k, 512:768], start=(k == 0), stop=(k == KC - 1)), [wk, ('uT', bi)], [PK(5)])
                    evac(xbst[:, 0:512], psf[4][:, 0:512], [PK(4)], ['rmat'])
                    evac(xbst[:, 512:768], psf[5][:, 0:256], [PK(5)], ['rmat'])
                    C.dma('sp', lambda e: e.dma_start(out=xbc_o[orow:orow + 128, 512 * g:512 * (g + 1)], in_=xbst[:, 0:512]), ['rmat'], [])
                    C.dma('sp', lambda e: e.dma_start(out=xbc_o[orow:orow + 128, 2048 + 128 * g:2048 + 128 * (g + 1)], in_=xbst[:, 512:640]), ['rmat'], [])
                    C.dma('sp', lambda e: e.dma_start(out=xbc_o[orow:orow + 128, 2560 + 128 * g:2560 + 128 * (g + 1)], in_=xbst[:, 640:768]), ['rmat'], [])
                if os.environ.get("KB") == "1" and g == 1 and bi == 0:
                    return stop_here()
                if os.environ.get("KB") == "2" and g == 1 and bi == 1:
                    return stop_here()
                for ci in range(n // 128):
                    l0 = ci * 128
                    t0 = b0 + l0
                    TRI_, STR_, ONE_ = (C_TRIB, C_STRICTB, C_ONESB) if samp else (C_TRI, C_STRICT, C_ONES)
                    for k in range(KC):
                        C.op('pe', lambda e: e.matmul(psf[0][:, 0:512], lhsT=uT[:, k, t0:t0 + 128], rhs=w[:, k, 768:1280], start=(k == 0), stop=(k == KC - 1)), [wk, ('uT', bi)], [PK(0)])
                    C.op('act', lambda e: e.activation(out=zs[:], in_=psf[0][:, 0:512], func=AF.Silu), [PK(0)], ['zs'])
                    tt_ = t0 // 128
                    dtt = dtt_a[:, tt_, :]; da = da_a[:, tt_, :]; ecs = ecs_a[:, tt_, :]; dtw = dtw_a[:, tt_, :]
                    edt = edt_a[:, tt_:tt_ + 4, :]
                    if os.environ.get("KCH") == "A" and g >= int(os.environ.get("KG", "0")):
                        return stop_here()
                    pb_ = ci % 2
                    for q in range(5):
                        C.op('pe', lambda e: e.transpose(psb[pb_][:, q * 128:(q + 1) * 128], xcT[:, q, l0:l0 + 128], ccb(C_ID)), ['xcT', 'cstb'], [PB(pb_)])
                    C.op('act', lambda e: e.activation(out=xs_tm[:], in_=psb[pb_][:, 0:512], func=AF.Copy), [PB(pb_)], ['xs_tm'])
                    C.op('act', lambda e: e.activation(out=b_tm[:], in_=psb[pb_][:, 512:640], func=AF.Copy), [PB(pb_)], ['b_tm'])
                    if os.environ.get("KCH") == "B" and g >= int(os.environ.get("KG", "0")):
                        return stop_here()
                    xs3 = xs_tm[:].rearrange("p (h d) -> p h d", d=64)
                    C.op('dve', lambda e: e.tensor_tensor(out=xdt[:].rearrange("p (h d) -> p h d", d=64), in0=xs3, in1=dtt.unsqueeze(2).to_broadcast([128, 8, 64]), op=ALU.mult),
                         ['xs_tm', 'dtt_a'], ['xdt'])
                    if os.environ.get("KCH") == "C" and g >= int(os.environ.get("KG", "0")):
                        return stop_here()
                    C.op('pool', lambda e: e.tensor_tensor(out=xdd[:].rearrange("p (h d) -> p h d", d=64), in0=xs3, in1=dtw.unsqueeze(2).to_broadcast([128, 8, 64]), op=ALU.mult),
                         ['xs_tm', 'dtw_a'], ['xdd'])
                    if os.environ.get("KCH") == "1" and g >= int(os.environ.get("KG", "0")):
                        return stop_here()
                    C.op('pe', lambda e: e.matmul(psf[3][:, 0:128], lhsT=xcT[:, 4, l0:l0 + 128], rhs=xcT[:, 5, l0:l0 + 128], start=True, stop=True), ['xcT'], [PK(3)])
                    C.op('dve', lambda e: e.tensor_tensor(out=cbm[:], in0=psf[3][:, 0:128], in1=cc(TRI_), op=ALU.mult), [PK(3), 'cst'], ['cbm'])
                    C.op(os.environ.get('KRM', 'pool'), lambda e: e.tensor_tensor(out=rmat[:], in0=da.unsqueeze(2).to_broadcast([128, 8, 128]), in1=cc(TRI_).unsqueeze(1).to_broadcast([128, 8, 128]), op=ALU.mult),
                         ['da_a', 'cst'], ['rmat'])
                    rm2 = rmat[:].rearrange("p h l -> p (h l)")
                    for hh in range(2):
                        C.op('pe', lambda e: e.matmul(psf[4 + hh][:, 0:512], lhsT=cc(STR_), rhs=rm2[:, 512 * hh:512 * (hh + 1)], start=True, stop=True), ['rmat', 'cst'], [PK(4 + hh)])
                        C.op('act', lambda e: e.activation(out=emat[:].rearrange("p h l -> p (h l)")[:, 512 * hh:512 * (hh + 1)], in_=psf[4 + hh][:, 0:512], func=AF.Exp), [PK(4 + hh)], ['emat'])
                    C.op('dve', lambda e: e.tensor_tensor(out=mT[:], in0=emat[:], in1=cbm[:].unsqueeze(1).to_broadcast([128, 8, 128]), op=ALU.mult), ['emat', 'cbm'], ['mT'])
                    if os.environ.get("KCH") == "2" and g >= int(os.environ.get("KG", "0")):
                        return stop_here()
                    for h in range(8):
                        C.op('pe', lambda e: e.matmul(psf[0][:, 64 * h:64 * h + 64], lhsT=mT[:, h, :], rhs=xdt[:, 64 * h:64 * h + 64], start=True, stop=True), ['mT', 'xdt'], [PK(0)])
                    if not samp:
                        C.op('pe', lambda e: e.matmul(psf[1][:, 0:512], lhsT=xcT[:, 5, l0:l0 + 128], rhs=hTb[:], start=True, stop=True), ['xcT', 'hTb'], [PK(1)])
                    else:
                        C.op('dve', lambda e: e.tensor_tensor(out=ctm[:], in0=xcT[:, 5, 0:128].unsqueeze(1).to_broadcast([128, 4, 128]),
                                                              in1=ccb(C_COLSEL, 512).rearrange("p (b l) -> p b l", b=4), op=ALU.mult), ['xcT', 'cstb'], ['ctm'])
                        for b in range(4):
                            C.op('pe', lambda e: e.matmul(psf[1][:, 0:512], lhsT=ctm[:, b, :], rhs=hTsb[:, b, :], start=(b == 0), stop=(b == 3)), ['ctm', 'hTsb'], [PK(1)])
                    if os.environ.get("KCH") == "3" and g >= int(os.environ.get("KG", "0")):
                        return stop_here()
                    yv3 = yv[:].rearrange("p (h d) -> p h d", d=64)
                    C.op('dve', lambda e: e.tensor_tensor(out=yv3, in0=psf[1][:, 0:512].rearrange("p (h d) -> p h d", d=64), in1=ecs.unsqueeze(2).to_broadcast([128, 8, 64]), op=ALU.mult),
                         [PK(1), 'ecs_a'], ['yv'])
                    C.op('dve', lambda e: e.tensor_tensor(out=yv[:], in0=yv[:], in1=psf[0][:, 0:512], op=ALU.add), ['yv', PK(0)], ['yv'])
                    C.op('pool', lambda e: e.tensor_tensor(out=y2[:].rearrange("p (h d) -> p h d", d=64), in0=xs3, in1=prow[:, R_DSK + 8 * g:R_DSK + 8 * g + 8].unsqueeze(2).to_broadcast([128, 8, 64]), op=ALU.mult),
                         ['xs_tm', 'prow'], ['y2'])
                    C.op('dve', lambda e: e.tensor_tensor(out=yv[:], in0=yv[:], in1=y2[:], op=ALU.add), ['yv', 'y2'], ['yv'])
                    C.op('dve', lambda e: e.tensor_tensor(out=yv[:], in0=yv[:], in1=zs[:], op=ALU.mult), ['yv', 'zs'], ['yv'])
                    C.op('act', lambda e: e.activation(out=ysq[:], in_=yv[:], func=AF.Square, accum_out=ss[:]), ['yv'], ['y2', 'ss'])
                    C.op('act', lambda e: e.activation(out=ss[:], in_=ss[:], func=AF.Sqrt, scale=1.0 / 512, bias=normst['eps'][:, 0:1]), ['ss', 'epsc'], ['ss'])
                    C.op('dve', lambda e: e.reciprocal(out=ss[:], in_=ss[:]), ['ss'], ['ss'])
                    C.op('dve', lambda e: e.scalar_tensor_tensor(out=ssm_tm[:], in0=yv[:], scalar=ss[:, 0:1], in1=prow[:, R_NSSM + 512 * g:R_NSSM + 512 * (g + 1)], op0=ALU.mult, op1=ALU.mult),
                         ['yv', 'ss', 'prow'], ['ssm_tm'])
                    if os.environ.get("KCH") == "4" and g >= int(os.environ.get("KG", "0")):
                        return stop_here()
                    pb2 = 1 - pb_
                    for q in range(4):
                        C.op('pe', lambda e: e.transpose(psb[pb2][:, q * 128:(q + 1) * 128], ssm_tm[:, q * 128:(q + 1) * 128], ccb(C_ID)), ['ssm_tm', 'cstb'], [PB(pb2)])
                    evac(ssmT[:, 4 * g:4 * g + 4, t0:t0 + 128], psb[pb2][:, 0:512].rearrange("p (q t) -> p q t", t=128), [PB(pb2)], [('ssmT', bi)])
                    if os.environ.get("KCH") == "5" and g >= int(os.environ.get("KG", "0")):
                        return stop_here()
                    if not samp:
                        C.op('pe', lambda e: e.matmul(psf[3][:, 0:512], lhsT=b_tm[:], rhs=xdd[:], start=True, stop=True), ['b_tm', 'xdd'], [PK(3)])
                        C.op('dve', lambda e: e.tensor_tensor(out=hT[:].rearrange("p (h d) -> p h d", d=64), in0=hT[:].rearrange("p (h d) -> p h d", d=64),
                                                              in1=edt[:, 0, :].unsqueeze(2).to_broadcast([128, 8, 64]), op=ALU.mult), ['hT', 'edt_a'], ['hT'])
                        C.op('dve', lambda e: e.tensor_tensor(out=hT[:], in0=hT[:], in1=psf[3][:, 0:512], op=ALU.add), ['hT', PK(3)], ['hT'])
                        C.op('act', lambda e: e.activation(out=hTb[:], in_=hT[:], func=AF.Copy), ['hT'], ['hTb'])
                    else:
                        for b in range(4):
                            pi = 3 if b % 2 == 0 else 4
                            C.op('dve', lambda e: e.tensor_scalar(out=xddm[:], in0=xdd[:], scalar1=cc(C_BLK + b, 1), scalar2=None, op0=ALU.mult), ['xdd', 'cst'], ['xddm'])
                            C.op('pe', lambda e: e.matmul(psf[pi][:, 0:512], lhsT=b_tm[:], rhs=xddm[:], start=True, stop=True), ['b_tm', 'xddm'], [PK(pi)])
                            C.op('dve', lambda e: e.tensor_tensor(out=hTs[:, b, :].rearrange("p (h d) -> p h d", d=64), in0=hTs[:, b, :].rearrange("p (h d) -> p h d", d=64),
                                                                  in1=edt[:, b, :].unsqueeze(2).to_broadcast([128, 8, 64]), op=ALU.mult), ['hTs', 'edt_a'], ['hTs'])
                            C.op('dve', lambda e: e.tensor_tensor(out=hTs[:, b, :], in0=hTs[:, b, :], in1=psf[pi][:, 0:512], op=ALU.add), ['hTs', PK(pi)], ['hTs'])
                    if os.environ.get("KCH") == "6" and g >= int(os.environ.get("KG", "0")):
                        return stop_here()
            if os.environ.get("KCH") == "7" and g >= int(os.environ.get("KG", "0")):
                return stop_here()
            C.dma('sp', lambda e: e.dma_start(out=hTp_o[:, 512 * g:512 * (g + 1)], in_=hT[:]), ['hT'], [])
            C.dma('sp', lambda e: e.dma_start(out=hTs_o[:, :, 512 * g:512 * (g + 1)].rearrange("b n f -> n b f"), in_=hTs[:]), ['hTs'], [])
    C.barrier()
    SSMT_ALL = [('ssmT', bi) for bi in range(len(blocks))]

    if STOP <= 3:
        return stop_here()
    mrgT = sb("mrgT", [128, KC, NT], BF16)
    with ExitStack() as ph:
        wm = [sb("wm%d" % i, [128, 24, 128], BF16, ph) for i in range(2)]
        sgs = sb("sgs", [128, QB], F32, ph)

        def mload(j):
            w = wm[j % 2]; kk = ('wm', j % 2)
            wload(w[:, 0:16, :], wos_d, 0, 16, 128 * j, 128, kk)
            wload(w[:, 16:24, :], w_in_d, 0, 8, GA0 + 1024 + 128 * j, 128, kk)
        mload(0)
        for j in range(8):
            if j + 1 < 8:
                mload(j + 1)
            w = wm[j % 2]; wk = ('wm', j % 2)
            for bi, (b0, n) in enumerate(blocks):
                p1, p3 = 2 * (bi % 2), 2 * (bi % 2) + 1
                for k in range(16):
                    C.op('pe', lambda e: e.matmul(psf[p1][:, 0:n], lhsT=w[:, k, :], rhs=ssmT[:, k, b0:b0 + n], start=(k == 0), stop=(k == 15)), [wk] + SSMT_ALL, [PK(p1)])
                for k in range(8):
                    C.op('pe', lambda e: e.matmul(psf[p3][:, 0:n], lhsT=w[:, 16 + k, :], rhs=uT[:, k, b0:b0 + n], start=(k == 0), stop=(k == 7)), [wk] + UT_ALL, [PK(p3)])
                C.op('act', lambda e: e.activation(out=sgs[:, 0:n], in_=psf[p3][:, 0:n], func=AF.Sigmoid), [PK(p3)], ['sgs'])
                C.op('dve', lambda e: e.tensor_tensor(out=mrgT[:, j, b0:b0 + n], in0=psf[p1][:, 0:n], in1=sgs[:, 0:n], op=ALU.mult), [PK(p1), 'sgs'], [('mrgT', bi)])
    C.barrier()
    release("ssmT")

    if STOP <= 5:
        return stop_here()
    ph2 = ExitStack()
    dbias = sb("dbias", [128, NQB, PTL, 16], F32, ph2)
    cnew = sb("cnew", [128, 16], F32, ph2)
    sbias = sb("sbias", [128, 4, NPG, 16], F32, ph2)
    idx = sb("idx", [128, NPAGE], I32, ph2)
    with ExitStack() as ph:
        wf = sb("wf", [128, KC, 16], BF16, ph)
        wload(wf, w_in_d, 0, KC, F0, 16, 'wf')
        t1 = sb("t1", [128, NTL, 16], F32, ph); t2 = sb("t2", [128, NTL, 16], F32, ph)
        for t in range(NTL):
            pi = t % 4
            for k in range(KC):
                C.op('pe', lambda e: e.matmul(psf[pi][:, 0:16], lhsT=uT[:, k, t * 128:(t + 1) * 128], rhs=wf[:, k, :], start=(k == 0), stop=(k == KC - 1)),
                     ['wf'] + UT_ALL, [PK(pi)])
            C.op('dve', lambda e: e.tensor_tensor(out=t1[:, t, :], in0=psf[pi][:, 0:16], in1=prow[:, R_BF:R_BF + 16], op=ALU.add), [PK(pi), 'prow'], ['t1'])
        C.op('dve', lambda e: e.tensor_scalar(out=t2[:], in0=t1[:], scalar1=-1.0, scalar2=None, op0=ALU.mult), ['t1'], ['t2'])
        C.op('dve', lambda e: e.tensor_tensor(out=t2[:], in0=t2[:], in1=t1[:], op=ALU.min), ['t1', 't2'], ['t2'])
        C.op('act', lambda e: e.activation(out=t2[:], in_=t2[:], func=AF.Exp), ['t2'], ['t2'])
        C.op('act', lambda e: e.activation(out=t2[:], in_=t2[:], func=AF.Ln, bias=1.0), ['t2'], ['t2'])
        C.op('dve', lambda e: e.scalar_tensor_tensor(out=lf_tm[:], in0=t1[:], scalar=0.0, in1=t2[:], op0=ALU.min, op1=ALU.subtract), ['t1', 't2'], ['lf_tm'])
        C.dma('sp', lambda e: e.dma_start(out=lf_o.rearrange("(t p) h -> p t h", p=128), in_=lf_tm[:]), ['lf_tm'], [])

        def scan_mid(buf_a, buf_b, ka, kb_, nmid, view):
            s = 1; cur, ck, oth, ok = buf_a, ka, buf_b, kb_
            while s < nmid:
                C.op('dve', lambda e: e.tensor_tensor(out=view(oth, s, nmid), in0=view(cur, s, nmid), in1=view(cur, 0, nmid - s), op=ALU.add), [ck], [ok])
                C.op('pool', lambda e: e.tensor_copy(out=view(oth, 0, s), in_=view(cur, 0, s)), [ck], [ok])
                cur, ck, oth, ok = oth, ok, cur, ck
                s *= 2
            return cur, ck
        wps = sb("wps", [128, PTL, 16], F32, ph); pta = sb("pta", [128, PTL, 16], F32, ph); ptb = sb("ptb", [128, PTL, 16], F32, ph)
        pt0 = sb("pt0", [128, PTL, 16], F32, ph)
        lfp = lf_tm[:, 0:PTL, :]
        C.op('pe', lambda e: e.matmul(psf[0][:, 0:PTL * 16], lhsT=cc(C_TRI), rhs=lfp, start=True, stop=True), ['lf_tm', 'cst'], [PK(0)])
        C.op('pe', lambda e: e.matmul(psf[1][:, 0:PTL * 16], lhsT=cc(C_ONES), rhs=lfp, start=True, stop=True), ['lf_tm', 'cst'], [PK(1)])
        C.op('act', lambda e: e.activation(out=wps[:], in_=psf[0][:, 0:PTL * 16].rearrange("p (t h) -> p t h", h=16), func=AF.Copy), [PK(0)], ['wps'])
        C.op('act', lambda e: e.activation(out=pta[:], in_=psf[1][:, 0:PTL * 16].rearrange("p (t h) -> p t h", h=16), func=AF.Copy), [PK(1)], ['pta'])
        C.op('dve', lambda e: e.tensor_copy(out=pt0[:], in_=pta[:]), ['pta'], ['pt0'])
        inc, ik = scan_mid(pta, ptb, 'pta', 'ptb', PTL, lambda b, lo, hi: b[:, lo:hi, :])
        C.op('dve', lambda e: e.tensor_tensor(out=wps[:], in0=wps[:], in1=inc[:], op=ALU.add), ['wps', ik], ['wps'])
        C.op('dve', lambda e: e.tensor_tensor(out=wps[:], in0=wps[:], in1=pt0[:], op=ALU.subtract), ['wps', 'pt0'], ['wps'])
        for qb in range(NQB):
            te = (qb + 1) * (QB // 128) - 1
            C.op('dve', lambda e: e.tensor_tensor(out=dbias[:, qb, :, :], in0=inc[:, te:te + 1, :].to_broadcast([128, PTL, 16]), in1=wps[:], op=ALU.subtract),
                 ['wps', ik], ['dbias'])
        C.op('pe', lambda e: e.matmul(psf[2][:, 0:16], lhsT=cc(C_TRIB), rhs=lf_tm[:, PTL, :], start=True, stop=True), ['lf_tm', 'cst'], [PK(2)])
        C.op('act', lambda e: e.activation(out=cnew[:], in_=psf[2][:, 0:16], func=AF.Copy, scale=-1.0), [PK(2)], ['cnew'])
        idr = sb("idr", [128, NPAGE], I32, ph)
        C.dma('sp', lambda e: e.dma_start(out=idr[:], in_=ptab_d.partition_broadcast(128)), (), ['idr'])
        C.op('dve', lambda e: e.tensor_scalar(out=idx[:], in0=idr[:], scalar1=128.0, scalar2=cc(C_IOTA, 1), op0=ALU.mult, op1=ALU.add), ['idr', 'cst'], ['idx'])
        PP = min(128, NPAGE); NGRP = (NPAGE + 127) // 128
        idp = sb("idp", [128, NGRP], I32, ph)
        for a in range(NGRP):
            C.dma('sp', lambda e: e.dma_start(out=idp[0:PP, a:a + 1], in_=ptab_d[0:1, a * PP:(a + 1) * PP].rearrange("o p -> p o")), (), ['idp'])
        lfP = sb("lfP", [128, NGRP, 128, 16], F32, ph); wP = sb("wP", [128, NGRP, 128, 16], F32, ph)
        d1 = sb("d1", [128, NGRP, 16], F32, ph)
        for a in range(NGRP):
            C.dma('pool', lambda e: e.indirect_dma_start(out=lfP[0:PP, a, :, :].rearrange("p k h -> p (k h)"), out_offset=None, in_=lfpool2_d,
                                                       in_offset=bass.IndirectOffsetOnAxis(ap=idp[0:PP, a:a + 1], axis=0)), ['idp'], ['lfP'])
        for a in range(NGRP):
            for h in range(16):
                C.op('dve', lambda e: e.tensor_tensor_scan(out=wP[0:PP, a, :, h], data0=cc(C_ONES, 128, 0, PP), data1=lfP[0:PP, a, :, h], initial=0.0, op0=ALU.mult, op1=ALU.add),
                     ['lfP', 'cst'], ['wP'])
            C.op('pe', lambda e: e.matmul(psf[0][0:PP, 16 * a:16 * a + 16], lhsT=cc(C_PGLT, PP, 0, PP), rhs=wP[0:PP, a, 127, :], start=True, stop=True), ['wP', 'cst'], [PK(0)])
            C.op('pe', lambda e: e.matmul(psf[1][0:PP, 16 * a:16 * a + 16], lhsT=cc(C_PGSAME, PP, 0, PP), rhs=wP[0:PP, a, 127, :], start=True, stop=True), ['wP', 'cst'], [PK(1)])
        C.op('act', lambda e: e.activation(out=d1[0:PP, :, :], in_=psf[0][0:PP, 0:16 * NGRP].rearrange("p (a h) -> p a h", h=16), func=AF.Copy), [PK(0)], ['d1'])
        C.op('dve', lambda e: e.tensor_tensor(out=d1[0:PP, :, :], in0=psf[1][0:PP, 0:16 * NGRP].rearrange("p (a h) -> p a h", h=16), in1=d1[0:PP, :, :], op=ALU.subtract), [PK(1), 'd1'], ['d1'])
        for a in range(NGRP):
            C.op('dve', lambda e: e.tensor_tensor(out=wP[0:PP, a, :, :], in0=d1[0:PP, a, :].unsqueeze(1).to_broadcast([PP, 128, 16]), in1=wP[0:PP, a, :, :], op=ALU.subtract), ['d1', 'wP'], ['wP'])
        sbv = sbias[:].rearrange("k b g h -> k (b g) h")
        for a in range(NGRP):
            for hq in range(4):
                pi = 2 + (a * 4 + hq) % 2
                for hh in range(4):
                    h = 4 * hq + hh
                    C.op('pe', lambda e: e.transpose(psf[pi][:, hh * PP:(hh + 1) * PP], wP[0:PP, a, :, h], cc(C_ID, PP, 0, PP)), ['wP', 'cst'], [PK(pi)])
                C.op('act', lambda e: e.activation(out=sbv[:, a * PP:(a + 1) * PP, 4 * hq:4 * hq + 4], in_=psf[pi][:, 0:4 * PP].rearrange("k (h p) -> k p h", h=4), func=AF.Copy), [PK(pi)], ['sbias'])
    C.barrier()

    if STOP <= 6:
        return stop_here()
    attnT = sb("attnT", [128, KC, NT], BF16)
    qTs = sb("qTs", [128, KC, 128], BF16); kTs = sb("kTs", [128, KC, 128], BF16); v_s = sb("v_s", [128, 1024], BF16)
    C.op('pool', lambda e: e.memset(attnT[:, :, T:NT], 0.0), (), ['attnT'])
    with ExitStack() as ph:
        wqkv = [sb("wqkv%d" % i, [128, KC, 384], BF16, ph) for i in range(2)]
        qT = sb("qT", [128, NT], BF16, ph); kT = sb("kT", [128, NT], BF16, ph)
        vaug = sb("vaug", [128, NTL, 2, 128], BF16, ph)
        kvst = sb("kvst", [128, 2, 256], F32, ph)
        pT = [sb("pTt%d" % i, [128, QB], BF16, ph) for i in range(3)]
        rec = sb("rec", [128, QB], F32, ph)
        C.op('pool', lambda e: e.memset(vaug[:, :, :, 64:128], 1.0), (), ['vaug'])

        def qload(hp):
            w = wqkv[hp % 2]; kk = ('wqkv', hp % 2)
            wload(w[:, :, 0:128], w_in_d, 0, KC, Q0 + 128 * hp, 128, kk)
            wload(w[:, :, 128:256], w_in_d, 0, KC, K0 + 128 * hp, 128, kk)
            wload(w[:, :, 256:384], w_in_d, 0, KC, V0 + 128 * hp, 128, kk)
        qload(0)
        for hp in range(8):
            if hp + 1 < 8:
                qload(hp + 1)
            w = wqkv[hp % 2]; wk = ('wqkv', hp % 2)
            for bi, (b0, n) in enumerate(blocks):
                fm_proj(w, wk, 0, uT, [('uT', bi)], KC, 0, b0, n)
                C.op('act', lambda e: e.activation(out=qT[:, b0:b0 + n], in_=psf[0][:, 0:n], func=AF.Copy, scale=0.125), [PK(0)], ['qT'])
                fm_proj(w, wk, 128, uT, [('uT', bi)], KC, 1, b0, n)
                C.op('dve', lambda e: e.tensor_scalar(out=kT[:, b0:b0 + n], in0=psf[1][:, 0:n], scalar1=1.0, scalar2=None, op0=ALU.mult), [PK(1)], ['kT'])
            KP4 = int(os.environ.get("KP4", "9"))
            for t in range(NTL if KP4 >= 2 else 0):
                pi = 2 + t % 2
                for k in range(KC):
                    C.op('pe', lambda e: e.matmul(psf[pi][:, 0:256], lhsT=uT[:, k, t * 128:(t + 1) * 128], rhs=w[:, k, 128:384], start=(k == 0), stop=(k == KC - 1)), [wk] + UT_ALL, [PK(pi)])
                C.op('act', lambda e: e.activation(out=kvst[:, t % 2, :], in_=psf[pi][:, 0:256], func=AF.Copy), [PK(pi)], [('kvst', t % 2)])
                if not os.environ.get("KNOVAUG"):
                    C.op('act', lambda e: e.activation(out=vaug[:, t, :, 0:64], in_=psf[pi][:, 128:256].rearrange("p (e d) -> p e d", d=64), func=AF.Copy), [PK(pi)], ['vaug'])
                if KP4 >= 3:
                    C.dma('sp', lambda e: e.dma_start(out=k_o[128 * t:128 * (t + 1), 128 * hp:128 * (hp + 1)], in_=kvst[:, t % 2, 0:128]), [('kvst', t % 2)], [])
                    C.dma('sp', lambda e: e.dma_start(out=v_o[128 * t:128 * (t + 1), 128 * hp:128 * (hp + 1)], in_=kvst[:, t % 2, 128:256]), [('kvst', t % 2)], [])
            if KP4 < 4:
                continue
            C.op('pool', lambda e: e.tensor_copy(out=qTs[:, hp, :], in_=qT[:, T:NT]), ['qT'], ['qTs'])
            C.op('pool', lambda e: e.tensor_copy(out=kTs[:, hp, :], in_=kT[:, T:NT]), ['kT'], ['kTs'])
            C.op('pool', lambda e: e.tensor_copy(out=v_s[:, 128 * hp:128 * (hp + 1)].rearrange("p (e d) -> p e d", d=64), in_=vaug[:, PTL, :, 0:64]), ['vaug'], ['v_s'])
            un = 0
            for qb in range(NQB if not os.environ.get("KSKIP_ATT") else 0):
                Q0_ = qb * QB
                for e_ in range(2):
                    h = 2 * hp + e_
                    r0, r1 = 64 * e_, 64 * e_ + 64
                    acc = e_
                    nkb = (Q0_ + QB) // 128
                    units = []
                    for kb in range(nkb):
                        qlo = max(Q0_, 128 * kb)
                        units.append((kb, qlo, Q0_ + QB - qlo, 2 + un % 3, un % 3)); un += 1

                    def emit_s(u):
                        kb, qlo, N, sbk, pk = u
                        pt_ = pT[pk]; ptk = ('pT', pk)
                        C.op('pe', lambda e: e.matmul(psf[sbk][:, 0:N], lhsT=kT[r0:r1, 128 * kb:128 * kb + 128], rhs=qT[r0:r1, qlo:qlo + N], start=True, stop=True), ['kT', 'qT'], [PK(sbk)])
                        C.op('act', lambda e: e.activation(out=pt_[:, 0:N], in_=psf[sbk][:, 0:N], func=AF.Exp, bias=dbias[:, qb, kb, h:h + 1]), [PK(sbk), 'dbias'], [ptk])
                        if 128 * kb >= Q0_:
                            C.op('pool', lambda e: e.tensor_tensor(out=pt_[:, 0:128], in0=pt_[:, 0:128], in1=ccb(C_TRI), op=ALU.mult), [ptk, 'cstb'], [ptk])

                    def emit_pv(u):
                        kb, qlo, N, sbk, pk = u
                        pt_ = pT[pk]; ptk = ('pT', pk)
                        C.op('pe', lambda e: e.matmul(psf[acc][:, qlo - Q0_:QB], lhsT=vaug[:, kb, e_, :], rhs=pt_[:, 0:N], start=(kb == 0), stop=(kb == nkb - 1)), [ptk, 'vaug'], [PK(acc)])
                    SK = 2
                    for i in range(len(units) + SK):
                        if i < len(units):
                            emit_s(units[i])
                        if i >= SK:
                            emit_pv(units[i - SK])
                    C.op('dve', lambda e: e.reciprocal(out=rec[64:128, :], in_=psf[acc][64:128, 0:QB]), [PK(acc)], ['rec'])
                    C.op('dve', lambda e: e.tensor_tensor(out=attnT[r0:r1, hp, Q0_:Q0_ + QB], in0=psf[acc][0:64, 0:QB], in1=rec[64:128, :], op=ALU.mult), [PK(acc), 'rec'], [('attnT', qb)])
    C.barrier()

    if STOP <= 7:
        return stop_here()
    with ExitStack() as ph:
        kpg = [sb("kpg%d" % i, [128, KC, 128], BF16, ph) for i in range(4)]
        vpg = [sb("vpg%d" % i, [128, 1024], BF16, ph) for i in range(4)]
        qblk = sb("qblk", [128, KC, 64], BF16, ph)
        stmp = [sb("stmp%d" % i, [128, 64], F32, ph) for i in range(2)]
        pts = [sb("pts%d" % i, [128, 64], BF16, ph) for i in range(2)]
        osel = sb("osel", [64, 16, 64], F32, ph); ored = sb("ored", [64, 64], F32, ph); den = sb("den", [64, 1], F32, ph)
        on = sb("on", [64, 64], BF16, ph); otr = sb("otr", [64, 64], BF16, ph)

        def pgload(j):
            C.dma('pool', lambda e: e.indirect_dma_start(out=kpg[j % 4][:].rearrange("p c k -> p (c k)"), out_offset=None, in_=kpool_d,
                                                       in_offset=bass.IndirectOffsetOnAxis(ap=idx[:, j:j + 1], axis=0)), ['idx'], [('kpg', j % 4)])
            C.dma('pool', lambda e: e.indirect_dma_start(out=vpg[j % 4][:], out_offset=None, in_=vpool_d,
                                                       in_offset=bass.IndirectOffsetOnAxis(ap=idx[:, j:j + 1], axis=0)), ['idx'], [('vpg', j % 4)])
        pgload(0)
        if NPAGE > 1:
            pgload(1)
        if NPAGE > 2:
            pgload(2)
        for b in range(4):
            C.op('pool', lambda e: e.memset(qblk[:], 0.0), (), ['qblk'])
            for h in range(16):
                c, r0 = h // 2, 64 * (h % 2)
                C.op('dve', lambda e: e.tensor_copy(out=qblk[r0:r0 + 64, c, 4 * h:4 * h + 4], in_=qTs[r0:r0 + 64, c, 32 * b:32 * b + 4]), ['qTs'], ['qblk'])
            for g_ in range(NPG + 1):
                j = b * NPG + g_
                new = (g_ == NPG)
                if not new and j + 3 < NPAGE:
                    pgload(j + 3)
                s_ = g_ % 2
                if not new:
                    for c in range(KC):
                        C.op('pe', lambda e: e.matmul(psf[3 + s_][:, 0:64], lhsT=kpg[j % 4][:, c, :], rhs=qblk[:, c, :], start=(c == 0), stop=(c == KC - 1)), [('kpg', j % 4), 'qblk'], [PK(3 + s_)])
                    C.op('dve', lambda e: e.tensor_tensor(out=stmp[s_][:].rearrange("p (h q) -> p h q", q=4), in0=psf[3 + s_][:, 0:64].rearrange("p (h q) -> p h q", q=4),
                                                          in1=sbias[:, b, g_, :].unsqueeze(2).to_broadcast([128, 16, 4]), op=ALU.add), [PK(3 + s_), 'sbias'], [('stmp', s_)])
                    C.op('act', lambda e: e.activation(out=pts[s_][:], in_=stmp[s_][:], func=AF.Exp), [('stmp', s_)], [('pts', s_)])
                    lh = pts[s_][:, :]; vr = vpg[j % 4]; vk = ('vpg', j % 4); ones_l = ccb(C_ONES, 1)
                else:
                    for c in range(KC):
                        C.op('pe', lambda e: e.matmul(psf[3 + s_][:, 0:64], lhsT=kTs[:, c, :], rhs=qblk[:, c, :], start=(c == 0), stop=(c == KC - 1)), ['kTs', 'qblk'], [PK(3 + s_)])
                    C.op('dve', lambda e: e.tensor_tensor(out=stmp[s_][:].rearrange("p (h q) -> p h q", q=4), in0=psf[3 + s_][:, 0:64].rearrange("p (h q) -> p h q", q=4),
                                                          in1=cnew[:].unsqueeze(2).to_broadcast([128, 16, 4]), op=ALU.add), [PK(3 + s_), 'cnew'], [('stmp', s_)])
                    C.op('act', lambda e: e.activation(out=pts[s_][:], in_=stmp[s_][:], func=AF.Exp), [('stmp', s_)], [('pts', s_)])
                    C.op('dve', lambda e: e.tensor_tensor(out=pts[s_][:], in0=pts[s_][:], in1=ccb(C_CAUS + 64 * b, 64), op=ALU.mult), [('pts', s_), 'cstb'], [('pts', s_)])
                    lh = pts[s_][:, :]; vk = 'v_s'; ones_l = ccb(C_ONES, 1)
                st_, sp_ = (g_ == 0), new
                if not new:
                    C.op('pe', lambda e: e.matmul(psf[0][0:64, 0:512], lhsT=lh, rhs=vpg[j % 4][:, 0:512], start=st_, stop=sp_), [('pts', s_), vk], [PK(0)])
                    C.op('pe', lambda e: e.matmul(psf[1][0:64, 0:512], lhsT=lh, rhs=vpg[j % 4][:, 512:1024], start=st_, stop=sp_), [('pts', s_), vk], [PK(1)])
                else:
                    C.op('pe', lambda e: e.matmul(psf[0][0:64, 0:512], lhsT=lh, rhs=v_s[:, 0:512], start=st_, stop=sp_), [('pts', s_), vk], [PK(0)])
                    C.op('pe', lambda e: e.matmul(psf[1][0:64, 0:512], lhsT=lh, rhs=v_s[:, 512:1024], start=st_, stop=sp_), [('pts', s_), vk], [PK(1)])
                C.op('pe', lambda e: e.matmul(psf[2][0:64, 0:1], lhsT=lh, rhs=ones_l, start=st_, stop=sp_), [('pts', s_), 'cstb'], [PK(2)])
            for hh in range(2):
                C.op('dve', lambda e: e.tensor_tensor(out=osel[:, 8 * hh:8 * hh + 8, :], in0=psf[hh][0:64, 0:512].rearrange("p (h d) -> p h d", d=64),
                                                      in1=cc(C_BD + 8 * hh, 8, 0, 64).unsqueeze(2).to_broadcast([64, 8, 64]), op=ALU.mult), [PK(hh), 'cst'], ['osel'])
            C.op('dve', lambda e: e.tensor_reduce(out=ored[:], in_=osel[:].rearrange("p h d -> p d h"), axis=AX.X, op=ALU.add), ['osel'], ['ored'])
            C.op('dve', lambda e: e.reciprocal(out=den[:], in_=psf[2][0:64, 0:1]), [PK(2)], ['den'])
            C.op('dve', lambda e: e.tensor_scalar(out=on[:], in0=ored[:], scalar1=den[:, 0:1], scalar2=None, op0=ALU.mult), ['ored', 'den'], ['on'])
            C.op('pe', lambda e: e.transpose(psb[0][0:64, 0:64], on[:], ccb(C_ID, 64, 0, 64)), ['on', 'cstb'], [PB(0)])
            C.op('act', lambda e: e.activation(out=otr[:], in_=psb[0][0:64, 0:64], func=AF.Copy), [PB(0)], ['otr'])
            for h in range(16):
                c, r0 = h // 2, 64 * (h % 2)
                C.op('dve', lambda e: e.tensor_copy(out=attnT[r0:r0 + 64, c, T + 32 * b:T + 32 * b + 4], in_=otr[:, 4 * h:4 * h + 4]), ['otr'], ['attnT'])
    ph2.close()
    C.barrier()
    ATT_ALL = ['attnT'] + [('attnT', qb) for qb in range(NQB)]

    if STOP <= 8:
        return stop_here()
    with ExitStack() as ph:
        wm = [sb("wmb%d" % i, [128, 16, 128], BF16, ph) for i in range(2)]
        sga = sb("sga", [128, QB], F32, ph); tma = sb("tma", [128, QB], F32, ph)

        def mload2(j):
            w = wm[j % 2]; kk = ('wmb', j % 2)
            wload(w[:, 0:8, :], woa_d, 0, 8, 128 * j, 128, kk)
            wload(w[:, 8:16, :], w_in_d, 0, 8, GA0 + 128 * j, 128, kk)
        mload2(0)
        for j in range(8):
            if j + 1 < 8:
                mload2(j + 1)
            w = wm[j % 2]; wk = ('wmb', j % 2)
            for bi, (b0, n) in enumerate(blocks):
                p0, p2 = 2 * (bi % 2), 2 * (bi % 2) + 1
                for k in range(8):
                    C.op('pe', lambda e: e.matmul(psf[p0][:, 0:n], lhsT=w[:, k, :], rhs=attnT[:, k, b0:b0 + n], start=(k == 0), stop=(k == 7)), [wk] + ATT_ALL, [PK(p0)])
                for k in range(8):
                    C.op('pe', lambda e: e.matmul(psf[p2][:, 0:n], lhsT=w[:, 8 + k, :], rhs=uT[:, k, b0:b0 + n], start=(k == 0), stop=(k == 7)), [wk] + UT_ALL, [PK(p2)])
                C.op('act', lambda e: e.activation(out=sga[:, 0:n], in_=psf[p2][:, 0:n], func=AF.Sigmoid), [PK(p2)], ['sga'])
                C.op('dve', lambda e: e.tensor_tensor(out=tma[:, 0:n], in0=psf[p0][:, 0:n], in1=sga[:, 0:n], op=ALU.mult), [PK(p0), 'sga'], ['tma'])
                C.op('pool', lambda e: e.tensor_tensor(out=mrgT[:, j, b0:b0 + n], in0=tma[:, 0:n], in1=mrgT[:, j, b0:b0 + n], op=ALU.add), ['tma', ('mrgT', bi)], [('mrgT', bi)])
    C.barrier()
    release("attnT"); release("uT"); release("qTs"); release("kTs"); release("v_s")
    MRG_ALL = [('mrgT', bi) for bi in range(len(blocks))]
    xres = sb("xres", [128, KC, NT], F32)

    def gen_sweep(wd, nk, rhsT, rhs_all, wtiles, wkeyname, post):
        def ld(j):
            wload(wtiles[j % 2], wd, 0, nk, 128 * j, 128, (wkeyname, j % 2))
        ld(0)
        un = 0
        for j in range(8):
            if j + 1 < 8:
                ld(j + 1)
            for bi, (b0, n) in enumerate(blocks):
                pi = un % 4; un += 1
                fm_proj(wtiles[j % 2], (wkeyname, j % 2), 0, rhsT, rhs_all, nk, pi, b0, n)
                post(j, bi, b0, n, pi)

    with ExitStack() as ph:
        wo = [sb("wo%d" % i, [128, 8, 128], BF16, ph) for i in range(2)]
        xs2 = [sb("xs2_%d" % i, [128, QB], F32, ph) for i in range(2)]
        cnt = [0]

        def post_out(j, bi, b0, n, pi):
            s_ = cnt[0] % 2; cnt[0] += 1
            C.dma('sp', lambda e: e.dma_start(out=xs2[s_][:, 0:n], in_=xT_d[128 * j:128 * (j + 1), b0:b0 + n]), (), [('xs2', s_)])
            C.op('dve', lambda e: e.tensor_tensor(out=xres[:, j, b0:b0 + n], in0=psf[pi][:, 0:n], in1=xs2[s_][:, 0:n], op=ALU.add), [PK(pi), ('xs2', s_)], [('xres', bi)])
        gen_sweep(wout_d, 8, mrgT, MRG_ALL, wo, 'wo', post_out)
    C.barrier()
    release("mrgT")
    XR_ALL = [('xres', bi) for bi in range(len(blocks))]

    if STOP <= 9:
        return stop_here()
    uT = sb("uT", [128, KC, NT], BF16)
    u2T = uT
    rms_to_bf(lambda bi, b0, n: (xres[:, :, b0:b0 + n], ('xres', bi)), None, P_GFFN, u2T, 'uT', normst)
    C.barrier()
    with ExitStack() as ph:
        HF = 11
        hidT = sb("hidT", [128, HF, NT], BF16, ph)
        wu = [sb("wu%d" % i, [128, KC, 256], BF16, ph) for i in range(2)]
        wd_ = [sb("wd%d" % i, [128, HF, 128], BF16, ph) for i in range(2)]
        apad = sb("apad", [128, 2 + QB], F32, ph); asp = sb("asp", [128, 4, 6], F32, ph)
        ct2 = sb("ct2", [128, QB], F32, ph); sl2 = sb("sl2", [128, QB], F32, ph)
        aust = sb("aust", [128, 128], F32, ph)
        cfw = lambda j, i: pfm[:, P_CFW + j * 3 + i:P_CFW + j * 3 + i + 1]

        def uload(jj):
            w = wu[jj % 2]; kk = ('wu', jj % 2)
            wload(w[:, :, 0:128], wup_d, 0, KC, 128 * jj, 128, kk)
            wload(w[:, :, 128:256], wup_d, 0, KC, DFF + 128 * jj, 128, kk)
        uload(0)
        for half in range(2):
            C.op('pool', lambda e: e.memset(hidT[:, :, T:NT], 0.0), (), [('hidT', len(blocks) - 1)])
            for jl in range(HF):
                jj = half * HF + jl
                if jj + 1 < FC:
                    uload(jj + 1)
                w = wu[jj % 2]; wk = ('wu', jj % 2)
                C.op('pool', lambda e: e.memset(apad[:, 0:2], 0.0), (), ['apad'])
                C.dma('sp', lambda e: e.dma_start(out=asp[:, :, 0:2], in_=cf0_d[128 * jj:128 * (jj + 1), :, :]), (), ['asp'])
                for bi, (b0, n) in enumerate(blocks):
                    samp = (b0 == T)
                    fm_proj(w, wk, 0, u2T, [('uT', bi)], KC, 0, b0, n)
                    fm_proj(w, wk, 128, u2T, [('uT', bi)], KC, 1, b0, n)
                    if not samp:
                        evac(apad[:, 2:2 + n], psf[0][:, 0:n], [PK(0)], ['apad'])
                        xin = lambda i: apad[:, i:i + n]
                        cto = ct2[:, 0:n]; slo = sl2[:, 0:n]; rk = 'apad'
                        vin = psf[1][:, 0:n]
                        ho = hidT[:, jl, b0:b0 + n]
                    else:
                        evac(asp[:, :, 2:6], psf[0][:, 0:128].rearrange("p (b t) -> p b t", t=32)[:, :, 0:4], [PK(0)], ['asp'])
                        xin = lambda i: asp[:, :, i:i + 4]
                        cto = ct2[:, 0:16].rearrange("p (b t) -> p b t", t=4); slo = sl2[:, 0:16].rearrange("p (b t) -> p b t", t=4); rk = 'asp'
                        vin = psf[1][:, 0:128].rearrange("p (b t) -> p b t", t=32)[:, :, 0:4]
                        ho = hidT[:, jl, T:NT].rearrange("p (b t) -> p b t", t=32)[:, :, 0:4]
                    C.op('dve', lambda e: e.tensor_scalar(out=cto, in0=xin(0), scalar1=cfw(jj, 0), scalar2=pfm[:, P_CFB + jj:P_CFB + jj + 1], op0=ALU.mult, op1=ALU.add), [rk, 'pfm'], ['ct2'])
                    for i in range(1, 3):
                        C.op('dve', lambda e: e.scalar_tensor_tensor(out=cto, in0=xin(i), scalar=cfw(jj, i), in1=cto, op0=ALU.mult, op1=ALU.add), [rk, 'ct2', 'pfm'], ['ct2'])
                    C.op('act', lambda e: e.activation(out=slo, in_=cto, func=AF.Silu), ['ct2'], ['sl2'])
                    C.op('dve', lambda e: e.tensor_tensor(out=ho, in0=vin, in1=slo, op=ALU.mult), [PK(1), 'sl2'], [('hidT', bi)])
                    if not samp:
                        C.op('pool', lambda e: e.tensor_copy(out=apad[:, 0:2], in_=apad[:, n:n + 2]), ['apad'], ['apad'])
                    if bi >= len(blocks) - 2:
                        tcol = b0 + n - 128
                        orow = 0 if not samp else 128
                        for k in range(KC):
                            C.op('pe', lambda e: e.matmul(psf[2][:, 0:128], lhsT=u2T[:, k, tcol:tcol + 128], rhs=w[:, k, 0:128], start=(k == 0), stop=(k == KC - 1)), [wk, ('uT', bi)], [PK(2)])
                        evac(aust[:], psf[2][:, 0:128], [PK(2)], ['aust'])
                        C.dma('sp', lambda e: e.dma_start(out=aup_o[orow:orow + 128, 128 * jj:128 * (jj + 1)], in_=aust[:]), ['aust'], [])
            HID_ALL = [('hidT', bi) for bi in range(len(blocks))]

            def dload(j):
                wload(wd_[j % 2], wdn_d, half * HF * 128, HF, 128 * j, 128, ('wd', j % 2))
            dload(0)
            for j in range(8):
                if j + 1 < 8:
                    dload(j + 1)
                for bi, (b0, n) in enumerate(blocks):
                    pi = 2 + (j * len(blocks) + bi) % 3
                    fm_proj(wd_[j % 2], ('wd', j % 2), 0, hidT, HID_ALL, HF, pi, b0, n)
                    C.op('dve', lambda e: e.tensor_tensor(out=xres[:, j, b0:b0 + n], in0=xres[:, j, b0:b0 + n], in1=psf[pi][:, 0:n], op=ALU.add), [PK(pi), ('xres', bi)], [('xres', bi)])
    C.barrier()

    if STOP <= 10:
        return stop_here()
    u3T = uT
    rms_to_bf(lambda bi, b0, n: (xres[:, :, b0:b0 + n], ('xres', bi)), None, P_GPLE, u3T, 'uT', normst)
    C.barrier()
    with ExitStack() as ph:
        wpg = [sb("wpg%d" % i, [128, 10, 128], BF16, ph) for i in range(2)]
        pTb = sb("pTb", [128, 2, NT], BF16, ph)
        sg3 = sb("sg3", [128, QB], F32, ph)
        C.dma('pool', lambda e: e.dma_start(out=pTb[:], in_=pT_d.rearrange("(k p) n -> p k n", p=128)), (), ['pTb'])

        def pload(j):
            wload(wpg[j % 2][:, 0:8, :], wpg_d, 0, 8, 128 * j, 128, ('wpg', j % 2))
            wload(wpg[j % 2][:, 8:10, :], wpi_d, 0, 2, 128 * j, 128, ('wpg', j % 2))
        pload(0)
        for j in range(8):
            if j + 1 < 8:
                pload(j + 1)
            w = wpg[j % 2]; wk = ('wpg', j % 2)
            for bi, (b0, n) in enumerate(blocks):
                p0, p1 = 2 * (bi % 2), 2 * (bi % 2) + 1
                for k in range(8):
                    C.op('pe', lambda e: e.matmul(psf[p0][:, 0:n], lhsT=w[:, k, :], rhs=u3T[:, k, b0:b0 + n], start=(k == 0), stop=(k == 7)), [wk, ('uT', bi)], [PK(p0)])
                for k in range(2):
                    C.op('pe', lambda e: e.matmul(psf[p1][:, 0:n], lhsT=w[:, 8 + k, :], rhs=pTb[:, k, b0:b0 + n], start=(k == 0), stop=(k == 1)), [wk, 'pTb'], [PK(p1)])
                C.op('act', lambda e: e.activation(out=sg3[:, 0:n], in_=psf[p0][:, 0:n], func=AF.Sigmoid), [PK(p0)], ['sg3'])
                C.op('dve', lambda e: e.tensor_tensor(out=sg3[:, 0:n], in0=psf[p1][:, 0:n], in1=sg3[:, 0:n], op=ALU.mult), [PK(p1), 'sg3'], ['sg3'])
                C.op('pool', lambda e: e.tensor_tensor(out=xres[:, j, b0:b0 + n], in0=xres[:, j, b0:b0 + n], in1=sg3[:, 0:n], op=ALU.add), ['sg3', ('xres', bi)], [('xres', bi)])
    C.barrier()
    with ExitStack() as ph:
        yst = [sb("yst%d" % i, [128, KC, QB], F32, ph) for i in range(2)]
        sqf = [sb("sqf%d" % i, [128, KC, QB], BF16, ph) for i in range(2)]
        yTv = yT_o.rearrange("(c p) n -> p c n", p=128)
        for bi, (b0, n) in enumerate(blocks):
            sq = sqf[bi % 2]; rs = normst['rs'][bi % 2]; pi = 4 + bi % 2
            C.op('act', lambda e: e.activation(out=sq[:, :, 0:n], in_=xres[:, :, b0:b0 + n], func=AF.Square), [('xres', bi)], [('sq', bi % 2)])
            for c in range(KC):
                C.op('pe', lambda e: e.matmul(psf[pi][:, 0:n], lhsT=ccb(C_ONES), rhs=sq[:, c, 0:n], start=(c == 0), stop=(c == KC - 1)), [('sq', bi % 2), 'cstb'], [PK(pi)])
            C.op('act', lambda e: e.activation(out=rs[:, 0:n], in_=psf[pi][:, 0:n], func=AF.Sqrt, scale=1.0 / D, bias=normst['eps'][:, 0:1]), [PK(pi), 'epsc'], [('rs', bi % 2)])
            C.op('dve', lambda e: e.reciprocal(out=rs[:, 0:n], in_=rs[:, 0:n]), [('rs', bi % 2)], [('rs', bi % 2)])
            for c in range(KC):
                eng = 'dve'
                C.op(eng, lambda e: e.scalar_tensor_tensor(out=yst[bi % 2][:, c, 0:n], in0=xres[:, c, b0:b0 + n], scalar=pfm[:, P_GFIN + c:P_GFIN + c + 1],
                                                           in1=rs[:, 0:n], op0=ALU.mult, op1=ALU.mult), [('xres', bi), ('rs', bi % 2), 'pfm'], [('yst', bi % 2)])
            C.dma('sp', lambda e: e.dma_start(out=yTv[:, :, b0:b0 + n], in_=yst[bi % 2][:, :, 0:n]), [('yst', bi % 2)], [])
    C.final()
    es.close()
    return nc


def make_consts(NPG):
    p = np.arange(128)[:, None]; l = np.arange(128)[None, :]
    c = np.zeros((128, C_W), np.float32)
    c[:, C_ID:C_ID + 128] = (p == l)
    c[:, C_TRI:C_TRI + 128] = (p <= l)
    c[:, C_STRICT:C_STRICT + 128] = (p > l)
    c[:, C_ONES:C_ONES + 128] = 1.0
    same = (p // 32) == (l // 32)
    c[:, C_TRIB:C_TRIB + 128] = same & (p <= l)
    c[:, C_STRICTB:C_STRICTB + 128] = same & (p > l)
    c[:, C_ONESB:C_ONESB + 128] = same
    c[:, C_VALID] = (np.arange(128) % 32) < 4
    r = np.arange(64)[:, None]; hh = np.arange(16)[None, :]
    c[0:64, C_BD:C_BD + 16] = ((r // 4) == hh)
    kq = np.arange(64)[None, :] % 4
    for b in range(4):
        c[:, C_CAUS + 64 * b:C_CAUS + 64 * (b + 1)] = ((p % 32) <= kq) & ((p % 32) < 4) & ((p // 32) == b)
        c[:, C_BLK + b] = (np.arange(128) // 32) == b
        c[:, C_COLSEL + 128 * b:C_COLSEL + 128 * (b + 1)] = ((l // 32) == b)
    c[:, C_IOTA] = np.arange(128)
    samepg = (p // NPG) == (l // NPG)
    c[:, C_PGLT:C_PGLT + 128] = samepg & (p < l)
    c[:, C_PGSAME:C_PGSAME + 128] = samepg
    return c


_NC_CACHE = {}


def run(inp, T, NPG, n_cores):
    f = lambda a: np.ascontiguousarray(np.asarray(a, dtype=np.float32))
    NT = T + 128
    NPOOL = inp['cache_k'].shape[1]
    key = (T, NPG, NPOOL)
    if key not in _NC_CACHE:
        _NC_CACHE[key] = build(T, NPG, NPOOL)
    nc = _NC_CACHE[key]
    xp = f(inp['x_prompt']); xs = f(inp['x_sample']); pp = f(inp['p_prompt'])[0]; ps_ = f(inp['p_sample'])[0]
    ck = f(inp['cache_k'])[0]; cv = f(inp['cache_v'])[0]; clf = f(inp['cache_logf'])[0]
    kpool = np.ascontiguousarray(ck.reshape(NPOOL, 128, 8, 128).transpose(0, 3, 2, 1)).reshape(NPOOL * 128, 1024)
    vpool = cv.reshape(NPOOL * 128, 1024)
    lfpool2 = clf.reshape(NPOOL, 2048)
    fm = lambda v: f(v).reshape(-1, 128).T
    pfm = np.zeros((128, P_W), np.float32)
    pfm[:, P_GMIX:P_GMIX + 8] = fm(inp['norm_mix'][0]); pfm[:, P_GFFN:P_GFFN + 8] = fm(inp['norm_ffn'][0])
    pfm[:, P_GPLE:P_GPLE + 8] = fm(inp['norm_ple'][0]); pfm[:, P_GFIN:P_GFIN + 8] = fm(inp['norm_final'])
    pfm[:, P_CSW:P_CSW + 96] = f(inp['conv_ssm_w'][0]).reshape(4, 24, 128).transpose(2, 1, 0).reshape(128, 96)
    pfm[:, P_CSB:P_CSB + 24] = fm(inp['conv_ssm_b'][0])
    pfm[:, P_CFW:P_CFW + 66] = f(inp['conv_ffn_w'][0]).reshape(3, 22, 128).transpose(2, 1, 0).reshape(128, 66)
    pfm[:, P_CFB:P_CFB + 22] = fm(inp['conv_ffn_b'][0])
    prow = np.concatenate([f(inp['b_f'][0]), f(inp['dt_bias'][0]), f(inp['a_log'][0]), f(inp['d_skip'][0]), f(inp['norm_ssm'][0])])[None, :]
    consts = make_consts(NPG)
    shared = {'kpool': kpool, 'vpool': vpool, 'lfpool2': lfpool2, 'consts': consts, 'pvec_fm': pfm, 'pvec_row': np.ascontiguousarray(prow),
              'w_in': f(inp['w_in'])[0], 'w_o_attn': f(inp['w_o_attn'])[0], 'w_o_ssm': f(inp['w_o_ssm'])[0], 'w_out': f(inp['w_out'])[0],
              'w_up': f(inp['w_up'])[0], 'w_down': f(inp['w_down'])[0], 'w_ple_gate': f(inp['w_ple_gate'])[0], 'w_ple_in': f(inp['w_ple_in'])[0]}
    sst = f(inp['state_ssm'])[0]; scs = f(inp['state_conv_ssm'])[0]; scf = f(inp['state_conv_ffn'])[0]
    ptab = np.asarray(inp['page_table']).astype(np.int32)
    in_maps = []
    for i in range(n_cores):
        xT = np.zeros((D, NT), np.float32); pT = np.zeros((DPLE, NT), np.float32)
        xT[:, :T] = xp[i].T; pT[:, :T] = pp[i].T
        for b in range(4):
            xT[:, T + 32 * b:T + 32 * b + 4] = xs[4 * i + b].T
            pT[:, T + 32 * b:T + 32 * b + 4] = ps_[4 * i + b].T
        m = dict(shared)
        m['xT'] = xT; m['pT'] = pT
        m['ptab'] = np.ascontiguousarray(ptab[4 * i:4 * i + 4].reshape(1, -1))
        m['hT0'] = np.ascontiguousarray(sst[4 * i:4 * i + 4].transpose(0, 3, 1, 2).reshape(4, 128, DI))
        m['cs0'] = np.ascontiguousarray(scs[4 * i:4 * i + 4].transpose(2, 0, 1))
        m['cf0'] = np.ascontiguousarray(scf[4 * i:4 * i + 4].transpose(2, 0, 1))
        in_maps.append(m)
    res = run_bass_kernel_spmd(nc, in_maps, core_ids=list(range(n_cores)))
    R = res.results
    B = n_cores; DB = 4 * n_cores
    y_p = np.zeros((B, T, D), np.float32); y_s = np.zeros((DB, 4, D), np.float32)
    k_p = np.zeros((1, B, T, HA, DH), np.float32); v_p = np.zeros_like(k_p); lf_p = np.zeros((1, B, T, HA), np.float32)
    k_s = np.zeros((1, DB, 4, HA, DH), np.float32); v_s = np.zeros_like(k_s); lf_s = np.zeros((1, DB, 4, HA), np.float32)
    h_p = np.zeros((1, B, HS, 64, NS), np.float32); h_s = np.zeros((1, DB, HS, 64, NS), np.float32)
    cs_p = np.zeros((1, B, 3, CD), np.float32); cs_s = np.zeros((1, DB, 3, CD), np.float32)
    cf_p = np.zeros((1, B, 2, DFF), np.float32); cf_s = np.zeros((1, DB, 2, DFF), np.float32)
    for i in range(n_cores):
        r = R[i]
        yT = r['yT']; y_p[i] = yT[:, :T].T
        k_p[0, i] = r['k_tm'][:T].reshape(T, HA, DH); v_p[0, i] = r['v_tm'][:T].reshape(T, HA, DH); lf_p[0, i] = r['logf'][:T]
        h_p[0, i] = r['hT_p'].reshape(128, HS, 64).transpose(1, 2, 0)
        cs_p[0, i] = r['xbc_tm'][125:128]; cf_p[0, i] = r['aup_tm'][126:128]
        for b in range(4):
            s = 4 * i + b; c0 = T + 32 * b
            y_s[s] = yT[:, c0:c0 + 4].T
            k_s[0, s] = r['k_tm'][c0:c0 + 4].reshape(4, HA, DH); v_s[0, s] = r['v_tm'][c0:c0 + 4].reshape(4, HA, DH); lf_s[0, s] = r['logf'][c0:c0 + 4]
            h_s[0, s] = r['hT_s'][b].reshape(128, HS, 64).transpose(1, 2, 0)
            cs_s[0, s] = r['xbc_tm'][128 + 32 * b + 1:128 + 32 * b + 4]; cf_s[0, s] = r['aup_tm'][128 + 32 * b + 2:128 + 32 * b + 4]
    return (y_p, y_s, k_p, v_p, lf_p, h_p, cs_p, cf_p, k_s, v_s, lf_s, h_s, cs_s, cf_s)


def kernel(**inputs):
    T = inputs['x_prompt'].shape[1]
    NPG = inputs['page_table'].shape[1]
    return run(inputs, T, NPG, 8)
```
